# Optimizing a Trainium2 kernel written in Bass

```python
import math
import jax, jax.numpy as jnp
from jax import lax
import numpy as np

D_MODEL = 4096
BATCH = 2
SEQ = 4096
DEPTH = 1

MEM_LEN = 256
SWA_HEADS = 16
SWA_KV_HEADS = 4
SWA_HEAD_DIM = 128
WINDOW = 128
BLOCK = 128
M_HEADS = 4
M_QK_DIM = 128
M_V_DIM = 256
M_CHUNK = 128
CONV_WIDTH = 4
X_HEADS = 4
X_HEAD_DIM = 256
NUM_BUCKETS = 32
MAX_DISTANCE = 128

SWA_W = SWA_HEADS * SWA_HEAD_DIM
SWA_KV_W = SWA_KV_HEADS * SWA_HEAD_DIM
M_QK_W = M_HEADS * M_QK_DIM
M_V_W = M_HEADS * M_V_DIM
X_W = X_HEADS * X_HEAD_DIM
MIX_W = SWA_W + M_V_W + X_W

IN_SPLITS = (
    SWA_W, SWA_KV_W, SWA_KV_W, SWA_W,
    M_QK_W, M_QK_W, M_V_W, M_HEADS, M_HEADS, M_V_W, M_V_W,
    X_W, X_W,
)
IN_W = sum(IN_SPLITS)

DEEPNORM_ALPHA = (2.0 * DEPTH) ** 0.25
DEEPNORM_BETA = (8.0 * DEPTH) ** -0.25
LN_EPS = 1e-5
HEAD_NORM_EPS = 1e-6

kernel_name = "hymba_swa_mlstm_memxattn_deepnorm"


def _split_points(sizes):
    pts, acc = [], 0
    for s in sizes[:-1]:
        acc += s
        pts.append(acc)
    return pts


def layer_norm(x, g, b):
    xf = x.astype(jnp.float32)
    mu = jnp.mean(xf, axis=-1, keepdims=True)
    var = jnp.mean(jnp.square(xf - mu), axis=-1, keepdims=True)
    return (xf - mu) * lax.rsqrt(var + LN_EPS) * g.astype(jnp.float32) + b.astype(jnp.float32)


def t5_bucket(dist):
    max_exact = NUM_BUCKETS // 2
    is_small = dist < max_exact
    ratio = jnp.maximum(dist, max_exact).astype(jnp.float32) / max_exact
    large = max_exact + (jnp.log(ratio) / math.log(MAX_DISTANCE / max_exact)
                         * (NUM_BUCKETS - max_exact)).astype(jnp.int32)
    large = jnp.minimum(large, NUM_BUCKETS - 1)
    return jnp.where(is_small, dist, large)


def causal_depthwise_conv(x, w, b):
    c = x.shape[-1]
    y = lax.conv_general_dilated(
        x, w[:, None, :].astype(x.dtype), window_strides=(1,),
        padding=((CONV_WIDTH - 1, 0),),
        dimension_numbers=("NWC", "WIO", "NWC"),
        feature_group_count=c)
    return y + b.astype(x.dtype)


def swa_attention(q, k, v, rel_bias, sinks):
    B, S = q.shape[0], q.shape[1]
    nb = S // BLOCK
    G = SWA_HEADS // SWA_KV_HEADS
    q = q.reshape(B, nb, BLOCK, SWA_KV_HEADS, G, SWA_HEAD_DIM)
    k = k.reshape(B, nb, BLOCK, SWA_KV_HEADS, SWA_HEAD_DIM)
    v = v.reshape(B, nb, BLOCK, SWA_KV_HEADS, SWA_HEAD_DIM)
    shift = lambda t: jnp.concatenate([jnp.zeros_like(t[:, :1]), t[:, :-1]], axis=1)
    kk = jnp.concatenate([shift(k), k], axis=2)
    vv = jnp.concatenate([shift(v), v], axis=2)
    s = jnp.einsum("bnqkgd,bnskd->bnkgqs", q, kk).astype(jnp.float32) * (SWA_HEAD_DIM ** -0.5)

    qi = jnp.arange(BLOCK, dtype=jnp.int32)[:, None]
    si = jnp.arange(2 * BLOCK, dtype=jnp.int32)[None, :]
    dist = qi + BLOCK - si
    in_win = (dist >= 0) & (dist < WINDOW)
    first = (jnp.arange(nb) == 0)[:, None, None]
    valid = in_win[None] & ~(first & (si < BLOCK)[None])

    bucket = t5_bucket(jnp.clip(dist, 0, MAX_DISTANCE - 1))
    bias = rel_bias[bucket].astype(jnp.float32)
    bias = bias.transpose(2, 0, 1).reshape(SWA_KV_HEADS, G, BLOCK, 2 * BLOCK)
    s = jnp.where(valid[None, :, None, None], s + bias, -jnp.inf)

    sink = sinks.astype(jnp.float32).reshape(SWA_KV_HEADS, G)[None, None, :, :, None, None]
    m = jnp.maximum(jnp.max(s, axis=-1, keepdims=True), sink)
    p = jnp.exp(s - m)
    denom = jnp.sum(p, axis=-1, keepdims=True) + jnp.exp(sink - m)
    o = jnp.einsum("bnkgqs,bnskd->bnqkgd", (p / denom).astype(vv.dtype), vv)
    return o.reshape(B, S, SWA_W)


def mlstm_chunkwise(q, k, v, i_pre, f_pre):
    B, S = q.shape[0], q.shape[1]
    nc, L = S // M_CHUNK, M_CHUNK
    to_chunks = lambda t: jnp.moveaxis(
        t.astype(jnp.float32).reshape((B, nc, L) + t.shape[2:]), 3, 1)
    q = to_chunks(q)
    k = to_chunks(k) * (M_QK_DIM ** -0.5)
    v = to_chunks(v)
    log_i = to_chunks(i_pre)
    log_f = jax.nn.log_sigmoid(to_chunks(f_pre))

    b = jnp.cumsum(log_f, axis=-1)
    g = b[..., -1]
    a = g[..., None] - b + log_i
    m_loc = jnp.max(a, axis=-1)
    w = jnp.exp(a - m_loc[..., None])
    C_loc = jnp.einsum("bhclv,bhcld->bhcvd", w[..., None] * v, k)
    n_loc = jnp.einsum("bhcl,bhcld->bhcd", w, k)

    def step(carry, xs):
        C, n, m = carry
        g_c, m_l, C_l, n_l = xs
        m_new = jnp.maximum(g_c + m, m_l)
        sp = jnp.exp(g_c + m - m_new)
        sl = jnp.exp(m_l - m_new)
        C_new = sp[..., None, None] * C + sl[..., None, None] * C_l
        n_new = sp[..., None] * n + sl[..., None] * n_l
        return (C_new, n_new, m_new), (C, n, m)

    init = (jnp.zeros((B, M_HEADS, M_V_DIM, M_QK_DIM), jnp.float32),
            jnp.zeros((B, M_HEADS, M_QK_DIM), jnp.float32),
            jnp.zeros((B, M_HEADS), jnp.float32))
    xs = (jnp.moveaxis(g, 2, 0), jnp.moveaxis(m_loc, 2, 0),
          jnp.moveaxis(C_loc, 2, 0), jnp.moveaxis(n_loc, 2, 0))
    _, (C_prev, n_prev, m_prev) = lax.scan(step, init, xs)
    C_prev = jnp.moveaxis(C_prev, 0, 2)
    n_prev = jnp.moveaxis(n_prev, 0, 2)
    m_prev = jnp.moveaxis(m_prev, 0, 2)

    causal = jnp.tril(jnp.ones((L, L), dtype=bool))
    D = b[..., :, None] - b[..., None, :] + log_i[..., None, :]
    D = jnp.where(causal, D, -jnp.inf)
    inter = b + m_prev[..., None]
    m_t = jnp.maximum(inter, jnp.max(D, axis=-1))
    Sm = jnp.einsum("bhctd,bhcsd->bhcts", q, k) * jnp.exp(D - m_t[..., None])
    w_inter = jnp.exp(inter - m_t)
    num = (jnp.einsum("bhcts,bhcsv->bhctv", Sm, v)
           + w_inter[..., None] * jnp.einsum("bhctd,bhcvd->bhctv", q, C_prev))
    den = jnp.sum(Sm, axis=-1) + w_inter * jnp.einsum("bhctd,bhcd->bhct", q, n_prev)
    h = num / jnp.maximum(jnp.abs(den), jnp.exp(-m_t))[..., None]
    return jnp.moveaxis(h, 1, 3).reshape(B, S, M_HEADS, M_V_DIM)


def setup_inputs(seed: int = 0) -> dict:
    key = jax.random.key(seed)
    ks = jax.random.split(key, 14)
    f32 = jnp.float32
    x = jax.random.normal(ks[0], (BATCH, SEQ, D_MODEL), f32)
    mem = jax.random.normal(ks[1], (BATCH, MEM_LEN, D_MODEL), f32)
    w_in = jax.random.normal(ks[2], (D_MODEL, IN_W), f32) * D_MODEL ** -0.5
    conv_w = jax.random.normal(ks[3], (CONV_WIDTH, 2 * M_QK_W), f32) * CONV_WIDTH ** -0.5
    conv_b = 0.01 * jax.random.normal(ks[4], (2 * M_QK_W,), f32)
    b_i = 0.1 * jax.random.normal(ks[5], (M_HEADS,), f32)
    b_f = 3.0 + 3.0 * jax.random.uniform(ks[6], (M_HEADS,), f32)
    m_norm_g = 1.0 + 0.02 * jax.random.normal(ks[7], (M_V_W,), f32)
    rel_bias = 0.1 * jax.random.normal(ks[8], (NUM_BUCKETS, SWA_HEADS), f32)
    sinks = jax.random.normal(ks[9], (SWA_HEADS,), f32)
    w_mem_kv = jax.random.normal(ks[10], (D_MODEL, 2 * X_W), f32) * D_MODEL ** -0.5
    w_out = jax.random.normal(ks[11], (MIX_W, D_MODEL), f32) * (MIX_W ** -0.5) * DEEPNORM_BETA
    ln_g = 1.0 + 0.02 * jax.random.normal(ks[12], (D_MODEL,), f32)
    ln_b = 0.02 * jax.random.normal(ks[13], (D_MODEL,), f32)
    return {"x": x, "mem": mem, "w_in": w_in, "conv_w": conv_w, "conv_b": conv_b,
            "b_i": b_i, "b_f": b_f, "m_norm_g": m_norm_g, "rel_bias": rel_bias,
            "sinks": sinks, "w_mem_kv": w_mem_kv, "w_out": w_out,
            "ln_g": ln_g, "ln_b": ln_b}


def reference(x, mem, w_in, conv_w, conv_b, b_i, b_f, m_norm_g, rel_bias, sinks,
              w_mem_kv, w_out, ln_g, ln_b):
    B, S, _ = x.shape
    for _layer in range(DEPTH):
        proj = jnp.einsum("bsd,de->bse", x, w_in)
        (a_q, a_k, a_v, a_z, m_q, m_k, m_v, m_i, m_f, m_o, m_z, c_q, c_z) = jnp.split(
            proj, _split_points(IN_SPLITS), axis=-1)

        a_out = swa_attention(
            a_q.reshape(B, S, SWA_HEADS, SWA_HEAD_DIM),
            a_k.reshape(B, S, SWA_KV_HEADS, SWA_HEAD_DIM),
            a_v.reshape(B, S, SWA_KV_HEADS, SWA_HEAD_DIM),
            rel_bias, sinks)
        a_out = a_out.astype(jnp.float32) * jax.nn.silu(a_z.astype(jnp.float32))

        qk = jax.nn.silu(causal_depthwise_conv(jnp.concatenate([m_q, m_k], axis=-1), conv_w, conv_b))
        mq, mk = jnp.split(qk, [M_QK_W], axis=-1)
        i_pre = m_i.astype(jnp.float32) + b_i.astype(jnp.float32)
        f_pre = m_f.astype(jnp.float32) + b_f.astype(jnp.float32)
        h = mlstm_chunkwise(
            mq.reshape(B, S, M_HEADS, M_QK_DIM), mk.reshape(B, S, M_HEADS, M_QK_DIM),
            m_v.reshape(B, S, M_HEADS, M_V_DIM), i_pre, f_pre)
        h = jax.nn.sigmoid(m_o.astype(jnp.float32)).reshape(B, S, M_HEADS, M_V_DIM) * h
        mu = jnp.mean(h, axis=-1, keepdims=True)
        var = jnp.mean(jnp.square(h - mu), axis=-1, keepdims=True)
        h = (h - mu) * lax.rsqrt(var + HEAD_NORM_EPS) * m_norm_g.astype(jnp.float32).reshape(M_HEADS, M_V_DIM)
        m_out = h.reshape(B, S, M_V_W) * jax.nn.silu(m_z.astype(jnp.float32))

        mkv = jnp.einsum("bmd,de->bme", mem, w_mem_kv)
        mem_k, mem_v = jnp.split(mkv, [X_W], axis=-1)
        mem_k = mem_k.reshape(B, MEM_LEN, X_HEADS, X_HEAD_DIM)
        mem_v = mem_v.reshape(B, MEM_LEN, X_HEADS, X_HEAD_DIM)
        cs = jnp.einsum("bqhd,bmhd->bhqm", c_q.reshape(B, S, X_HEADS, X_HEAD_DIM),
                        mem_k).astype(jnp.float32) * (X_HEAD_DIM ** -0.5)
        cp = jax.nn.softmax(cs, axis=-1)
        c_out = jnp.einsum("bhqm,bmhd->bqhd", cp.astype(mem_v.dtype), mem_v).reshape(B, S, X_W)
        c_out = c_out.astype(jnp.float32) * jax.nn.silu(c_z.astype(jnp.float32))

        mix = jnp.concatenate([a_out, m_out, c_out], axis=-1).astype(x.dtype)
        y = jnp.einsum("bse,ed->bsd", mix, w_out)
        x = layer_norm(DEEPNORM_ALPHA * x.astype(jnp.float32) + y.astype(jnp.float32),
                       ln_g, ln_b).astype(x.dtype)
    return x
```

```python
import numpy as np
from contextlib import ExitStack
import concourse.bass as bass
import concourse.mybir as mybir
from concourse.bass_utils import run_bass_kernel_spmd

F32 = mybir.dt.float32
BF16 = mybir.dt.bfloat16
AF = mybir.ActivationFunctionType
ALU = mybir.AluOpType
AX = mybir.AxisListType

DEBUG = False
STOP = None


class _Stop(Exception):
    pass
T = 1024
H = 128
TH = T + H
NEG = -1.0e30
NCB = 96
O_AQ, O_AK, O_AV, O_AZ = 0, 2048, 2560, 3072
O_MQ, O_MK, O_MV, O_MI, O_MO, O_MZ = 5120, 5632, 6144, 7168, 7176, 8200
O_CQ, O_CZ = 9224, 10248
IN_W = 11272


class Sched:
    def __init__(self, nc, es):
        self.nc = nc
        self.es = es
        self.engs = {"pe": nc.tensor, "act": nc.scalar, "dve": nc.vector,
                     "pool": nc.gpsimd, "sp": nc.sync}
        self.esem = {}
        self.ecnt = {}
        for n in self.engs:
            self.esem[n] = es.enter_context(nc.semaphore("s_" + n))
            self.ecnt[n] = 0
        self.seen = {n: {} for n in self.engs}
        self.state = {}
        self.dsem = {}
        self.dcnt = {}
        self.flip = 0

    def _tok_wait(self, en, tok):
        if tok is None:
            return
        sem, val = tok
        if en == "pe" and sem is self.esem["pe"]:
            return
        k = id(sem)
        if self.seen[en].get(k, 0) >= val:
            return
        self.engs[en].wait_ge(sem, val)
        self.seen[en][k] = val

    def _deps(self, en, reads, writes, skip_sem=None):
        for k in reads:
            st = self.state.get(k)
            if st:
                self._tok_wait(en, st[0])
        for k in writes:
            st = self.state.get(k)
            if st:
                if not (skip_sem is not None and st[0] is not None and st[0][0] is skip_sem):
                    self._tok_wait(en, st[0])
                for t in st[1].values():
                    self._tok_wait(en, t)

    def _commit(self, tok, reads, writes):
        key = id(tok[0])
        for k in reads:
            st = self.state.setdefault(k, [None, {}])
            if key not in st[1] or st[1][key][1] < tok[1]:
                st[1][key] = tok
        for k in writes:
            self.state[k] = [tok, {}]

    PSK = {"pp0", "pp1", "pp2", "pp3", "pa0", "pa1", "psb", "po"}

    def op(self, en, fn, reads=(), writes=(), inc=True):
        pr = [k for k in reads if k in self.PSK]
        if pr:
            reads = [k for k in reads if k not in self.PSK]
            writes = list(writes) + [k for k in pr if k not in writes]
        self._deps(en, reads, writes)
        ins = fn(self.engs[en])
        if inc:
            self.ecnt[en] += 1
            ins.then_inc(self.esem[en], 1)
            tok = (self.esem[en], self.ecnt[en])
        else:
            tok = (self.esem[en], self.ecnt[en] + 1)
        self._commit(tok, reads, writes)
        return ins

    def any(self, fn, reads=(), writes=()):
        self.flip ^= 1
        return self.op("act" if self.flip else "dve", fn, reads, writes)

    def copy(self, out, in_, reads=(), writes=()):
        self.flip ^= 1
        if self.flip:
            return self.op("act", lambda e: e.copy(out=out, in_=in_), reads, writes)
        return self.op("dve", lambda e: e.tensor_copy(out=out, in_=in_), reads, writes)

    def getsem(self, sem):
        if sem not in self.dsem:
            self.dsem[sem] = self.es.enter_context(self.nc.semaphore("d_" + sem))
            self.dcnt[sem] = 0
        return self.dsem[sem]

    def dma(self, q, out, in_, reads=(), writes=(), sem=None):
        h = self.getsem(sem)
        self._deps(q, reads, writes, skip_sem=h)
        ins = self.engs[q].dma_start(out=out, in_=in_)
        self.dcnt[sem] += 16
        ins.then_inc(h, 16)
        tok = (h, self.dcnt[sem])
        self._commit(tok, reads, writes)
        return ins

    def barrier(self, engines=("pe", "act", "dve", "sp")):
        for e in engines:
            for o in engines:
                if o != e and self.ecnt[o] > 0:
                    self._tok_wait(e, (self.esem[o], self.ecnt[o]))
            for sem, h in self.dsem.items():
                if self.dcnt[sem] > 0:
                    self._tok_wait(e, (h, self.dcnt[sem]))

    def finish(self, en="sp"):
        for sem, h in self.dsem.items():
            self._tok_wait(en, (h, self.dcnt[sem]))
        for n in self.engs:
            if n != en and self.ecnt[n] > 0:
                self._tok_wait(en, (self.esem[n], self.ecnt[n]))


def build_program():
    nc = bass.Bass("TRN2", target_bir_lowering=False)

    def din(name, shape, dt=F32):
        return nc.dram_tensor(name, shape, dt, kind="ExternalInput").ap()

    xT_d = din("xT", [128, 32, TH])
    xres_d = din("xres", [T, 4096])
    xprev_d = din("xprev", [3, 128, 32, TH])
    memT_d = din("memT", [128, 32, 256])
    win_d = din("w_in_t", [128, 32, IN_W])
    wkv_d = din("w_kv_t", [128, 32, 2048])
    wout_d = din("w_out_t", [128, 32, 4096])
    cb_d = din("cb", [128, NCB])
    wg_d = din("wg", [128, 32, 8])
    gv_d = din("gvec", [128, 1024])
    lng_d = din("lng", [128, 4096])
    lnb_d = din("lnb", [128, 4096])
    bias_d = din("bias", [128, 16, 256])
    mats_d = din("mats", [128, 5, 128])
    masks_d = din("masks", [128, 2, 256])
    out_d = nc.dram_tensor("out", [T, 4096], F32, kind="ExternalOutput").ap()
    mix_d = nc.dram_tensor("mixT_s", [128, 32, T], BF16,
                           kind="ExternalOutput" if DEBUG else "Internal").ap()
    zscr_d = nc.dram_tensor("zscr", [T, 4096], F32).ap()

    try:
      with ExitStack() as es:
        S = Sched(nc, es)

        def stop_if(tag):
            if STOP == tag:
                S.barrier(("pe", "act", "dve", "sp", "pool"))
                S.finish("sp")
                raise _Stop()

        uid = [0]

        def sb(stack, name, shape, dt):
            uid[0] += 1
            return stack.enter_context(nc.sbuf_tensor(f"sb{uid[0]}_{name}", shape, dt))

        xT = sb(es, "xT", [128, 32, TH], BF16)
        NW = 2
        wbufs = [sb(es, f"wb{i}", [128, 32, 256], BF16) for i in range(NW)]
        wg = sb(es, "wg", [128, 32, 8], BF16)
        cb = sb(es, "cb", [128, NCB], F32)
        mats = sb(es, "mats", [128, 5, 128], F32)
        identb = sb(es, "identb", [128, 128], BF16)
        ident = mats[:, 0, :]
        ones = mats[:, 1, :]
        Umat = mats[:, 2, :]
        cmask = mats[:, 3, :]
        cmaskT = mats[:, 4, :]
        pp = [es.enter_context(nc.psum_tensor(f"pp{i}", [128, 512], F32)) for i in range(4)]
        pa = [es.enter_context(nc.psum_tensor(f"pa{i}", [128, 512], F32)) for i in range(2)]
        psb = es.enter_context(nc.psum_tensor("psb", [128, 1024], BF16))
        po = es.enter_context(nc.psum_tensor("po", [128, 512], F32))
        ppi = [0]

        def next_pp():
            i = ppi[0] % 4
            ppi[0] += 1
            return pp[i], f"pp{i}"

        wli = [0]

        def load_w(parts):
            slot = wli[0] % NW
            wli[0] += 1
            wb = wbufs[slot]
            off = 0
            for (src, c0, n) in parts:
                for k0 in range(0, 32, 8):
                    S.dma("pool", wb[:, k0:k0 + 8, off:off + n], src[:, k0:k0 + 8, c0:c0 + n],
                          writes=[f"w{slot}"], sem=f"w{slot}")
                off += n
            return wb, f"w{slot}"

        def mm_group_g(ps_ap, pkey, lhs_fn, rhs_fn, reads, step=8):
            for k in range(32):
                rk = [(f"xT{k // 4}" if r_ == "xT" else r_) for r_ in reads]
                S.op("pe", lambda e: e.matmul(ps_ap, lhsT=lhs_fn(k), rhs=rhs_fn(k),
                                              start=(k == 0), stop=(k == 31)),
                     reads=rk, writes=[pkey], inc=(k == 31))
                if k % step == step - 1 and k != 31:
                    yield

        def run(g):
            for _ in g:
                pass

        def mm_group(ps_ap, pkey, lhs_fn, rhs_fn, reads):
            run(mm_group_g(ps_ap, pkey, lhs_fn, rhs_fn, reads))

        def proj_fm_g(wb, wkey, col_offs, src, srckey, segs, evac):
            for mi, off in enumerate(col_offs):
                for si, (t0, n) in enumerate(segs):
                    ps, pkey = next_pp()
                    yield from mm_group_g(ps[:, 0:n], pkey, lambda k: wb[:, k, off:off + 128],
                                          lambda k: src[:, k, t0:t0 + n], [wkey, srckey])
                    evac(mi, si, ps[:, 0:n], pkey, t0, n)
                    yield

        def proj_fm(*a):
            run(proj_fm_g(*a))

        def proj_tm_g(wb, wkey, ncols, src, srckey, toffs, evac):
            for ti, t0 in enumerate(toffs):
                ps, pkey = next_pp()
                yield from mm_group_g(ps[:, 0:ncols], pkey, lambda k: src[:, k, t0:t0 + 128],
                                      lambda k: wb[:, k, 0:ncols], [wkey, srckey])
                evac(ti, ps[:, 0:ncols], pkey)
                yield

        def proj_tm(*a):
            run(proj_tm_g(*a))

        sg_tmp = {}

        def gate_evac(kind, dst, dkey, ps, pkey, n):
            i = sg_tmp["i"] = (sg_tmp.get("i", 0) + 1) % 2
            e_, x_ = sg_tmp["e"][i], sg_tmp["x"][i]
            ke, kx = f"sge{i}", f"sgx{i}"
            S.op("act", lambda e: e.activation(out=e_[:, 0:n], in_=ps, func=AF.Exp), reads=[pkey], writes=[ke])
            S.op("act", lambda e: e.copy(out=x_[:, 0:n], in_=ps), reads=[pkey], writes=[kx])
            S.op("act", lambda e: e.activation(out=e_[:, 0:n], in_=e_[:, 0:n], func=AF.Ln, bias=mats[:, 1, 0:1]),
                 reads=[ke, "mats"], writes=[ke])
            S.op("dve", lambda e: e.tensor_tensor(out=e_[:, 0:n], in0=x_[:, 0:n], in1=e_[:, 0:n], op=ALU.subtract),
                 reads=[ke, kx], writes=[ke])
            if kind == "silu":
                S.op("act", lambda e: e.activation(out=e_[:, 0:n], in_=e_[:, 0:n], func=AF.Exp), reads=[ke], writes=[ke])
                S.op("dve", lambda e: e.tensor_tensor(out=dst, in0=e_[:, 0:n], in1=x_[:, 0:n], op=ALU.mult),
                     reads=[ke, kx], writes=[dkey])
            else:
                S.op("act", lambda e: e.activation(out=dst, in_=e_[:, 0:n], func=AF.Exp), reads=[ke], writes=[dkey])

        def alloc_sg(stack, n):
            sg_tmp["e"] = [sb(stack, f"sge{i}", [128, n], F32) for i in range(2)]
            sg_tmp["x"] = [sb(stack, f"sgx{i}", [128, n], F32) for i in range(2)]

        class Pump:
            def __init__(self):
                self.g = None

            def set(self, g):
                self.g = g

            def __call__(self, n=1):
                for _ in range(n):
                    if self.g is None:
                        return
                    try:
                        next(self.g)
                    except StopIteration:
                        self.g = None
                        return

            def drain(self):
                while self.g is not None:
                    self(1)

        pump = Pump()

        SEGS_H = [(0, 512), (512, 512), (1024, 128)]
        SEGS = [(H, 512), (H + 512, 512)]

        S.dma("sp", cb[:], cb_d, writes=["cb"], sem="c_cb")
        S.dma("sp", mats[:], mats_d, writes=["mats"], sem="c_mats")
        S.dma("pool", wg[:], wg_d, writes=["wg"], sem="wg")
        S.op("dve", lambda e: e.tensor_copy(out=identb[:], in_=ident), reads=["mats"], writes=["identb"])
        stop_if("S0")

        def softmax_pv(L, slot, ps_s, pskey, bias_ap, bias_keys, extra_ap, sink_col, v_fn, vkeys,
                       gate_fn, gkeys, out_fn, okeys, n_dc):
            sm_part1(L, slot, ps_s, pskey, bias_ap, bias_keys, extra_ap, sink_col)
            sm_part2(L, slot, v_fn, vkeys, gate_fn, gkeys, out_fn, okeys, n_dc, 0)

        def sm_part1(L, slot, ps_s, pskey, bias_ap, bias_keys, extra_ap, sink_col):
            Sb, Pn, PT, st = L["Sb"][slot], L["Pn"][slot], L["PT"][slot], L["st"][slot]
            kS, kP, kT_, kst = f"Sb{slot}", f"Pn{slot}", f"PT{slot}", f"st{slot}"
            if bias_ap is not None:
                S.op("dve", lambda e: e.tensor_tensor(out=Sb[:], in0=ps_s, in1=bias_ap, op=ALU.add),
                     reads=[pskey] + bias_keys, writes=[kS])
                if extra_ap is not None:
                    S.op("dve", lambda e: e.tensor_tensor(out=Sb[:], in0=Sb[:], in1=extra_ap, op=ALU.add),
                         reads=[kS, "masks"], writes=[kS])
                src, srck = Sb[:], kS
            else:
                src, srck = ps_s, pskey
            S.op("dve", lambda e: e.tensor_reduce(out=st[:, 0:1], in_=src, axis=AX.X, op=ALU.max),
                 reads=[srck], writes=[kst])
            if sink_col is not None:
                S.op("dve", lambda e: e.tensor_scalar(out=st[:, 1:2], in0=st[:, 0:1], scalar1=sink_col,
                                                      scalar2=-1.0, op0=ALU.max, op1=ALU.mult),
                     reads=[kst, "cb"], writes=[kst])
            else:
                S.op("dve", lambda e: e.tensor_scalar(out=st[:, 1:2], in0=st[:, 0:1], scalar1=-1.0,
                                                      scalar2=None, op0=ALU.mult),
                     reads=[kst], writes=[kst])
            S.op("act", lambda e: e.activation(out=Sb[:], in_=src, func=AF.Exp, bias=st[:, 1:2],
                                               accum_out=st[:, 2:3]),
                 reads=[srck, kst], writes=[kS, kst])
            if sink_col is not None:
                S.op("act", lambda e: e.activation(out=st[:, 3:4], in_=sink_col, func=AF.Exp, bias=st[:, 1:2]),
                     reads=[kst, "cb"], writes=[kst])
                S.op("dve", lambda e: e.tensor_tensor(out=st[:, 2:3], in0=st[:, 2:3], in1=st[:, 3:4], op=ALU.add),
                     reads=[kst], writes=[kst])
            S.op("dve", lambda e: e.reciprocal(out=st[:, 5:6], in_=st[:, 2:3]), reads=[kst], writes=[kst])
            S.op("dve", lambda e: e.tensor_scalar(out=Pn[:], in0=Sb[:], scalar1=st[:, 5:6], scalar2=None,
                                                  op0=ALU.mult),
                 reads=[kS, kst], writes=[kP])

        def sm_part2(L, slot, v_fn, vkeys, gate_fn, gkeys, out_fn, okeys, n_dc, npump):
            Sb, Pn, PT, st = L["Sb"][slot], L["Pn"][slot], L["PT"][slot], L["st"][slot]
            kS, kP, kT_, kst = f"Sb{slot}", f"Pn{slot}", f"PT{slot}", f"st{slot}"
            pump(npump)
            pb = psb[:, slot * 256:(slot + 1) * 256]
            pbk = "psb"
            for hf in range(2):
                S.op("pe", lambda e: e.transpose(out=pb[:, hf * 128:(hf + 1) * 128],
                                                 in_=Pn[:, hf * 128:(hf + 1) * 128], identity=identb[:]),
                     reads=[kP, "identb"], writes=[pbk])
            S.copy(PT[:], pb, reads=[pbk], writes=[kT_])
            pump(npump)
            for dc in range(n_dc):
                oslot = (slot * 2 + dc) % 4
                pov = po[:, oslot * 128:(oslot + 1) * 128]
                pok = "po"
                for hf in range(2):
                    S.op("pe", lambda e: e.matmul(pov, lhsT=v_fn(dc, hf), rhs=PT[:, hf * 128:(hf + 1) * 128],
                                                  start=(hf == 0), stop=(hf == 1)),
                         reads=[kT_] + vkeys, writes=[pok], inc=(hf == 1))
                S.op("dve", lambda e: e.tensor_tensor(out=out_fn(dc), in0=pov, in1=gate_fn(dc), op=ALU.mult),
                     reads=[pok] + gkeys, writes=okeys)

        def attn_scratch(stack):
            return {
                "Sb": [sb(stack, f"Sb{i}", [128, 256], F32) for i in range(2)],
                "Pn": [sb(stack, f"Pn{i}", [128, 256], BF16) for i in range(2)],
                "PT": [sb(stack, f"PT{i}", [128, 256], BF16) for i in range(2)],
                "st": [sb(stack, f"st{i}", [128, 8], F32) for i in range(2)],
            }

        sums = sb(es, "sums", [128, 3, 1036], F32)

        def alloc_m(ph):
            vaug = sb(ph, "vaug", [128, 8, 4, 257], BF16)
            kT = sb(ph, "kT", [128, 4, T], BF16)
            ktm = sb(ph, "ktm", [128, 8, 4, 128], BF16)
            qT = sb(ph, "qT", [128, 4, T], BF16)
            gi = sb(ph, "gi", [128, 32], F32)
            gf = sb(ph, "gf", [128, 32], F32)
            bq = sb(ph, "bq", [128, 32], F32)
            grep = sb(ph, "grep", [128, 32], F32)
            apr = sb(ph, "apr", [128, 32], F32)
            mx = sb(ph, "mx", [128, 32], F32)
            cm = sb(ph, "cm", [128, 32], F32)
            Rs = sb(ph, "Rs", [128, 32], F32)
            tmp32 = sb(ph, "tmp32", [128, 32], F32)
            wts = sb(ph, "wts", [128, 32], F32)
            mseg = sb(ph, "mseg", [128, 4], F32)
            return (vaug, kT, ktm, qT, gi, gf, bq, grep, apr, mx, cm, Rs, tmp32, wts, mseg)

        def m1_pass(Bf, prev_i):
            (vaug, kT, ktm, qT, gi, gf, bq, grep, apr, mx, cm, Rs, tmp32, wts, mseg) = Bf
            with ExitStack() as p1:
                pre = [sb(p1, f"pre{i}", [128, TH], F32) for i in range(2)]
                acc = [sb(p1, f"acc{i}", [128, T], F32) for i in range(2)]
                dA = [sb(p1, f"dA{i}", [128, 4, 128], F32) for i in range(2)]
                tA = [sb(p1, f"tA{i}", [128, 4, 128], F32) for i in range(2)]
                wv = [sb(p1, f"wv{i}", [128, 4, 257], BF16) for i in range(2)]
                S.op("dve", lambda e: e.memset(vaug[:].rearrange("p a b c -> p (a b c)"), 1.0), writes=["vaug"])
                pg = pp[0]
                ppi[0] = 1
                for c in range(8):
                    mm_group(pg[:, c * 8:(c + 1) * 8], "pp0", lambda k: xT[:, k, H + c * 128:H + (c + 1) * 128],
                             lambda k: wg[:, k, :], ["wg", "xT"])
                def conv_proj(o_col, dst, dkey, cbase, scale):
                    for t in range(2):
                        wb, wk = load_w([(win_d, o_col + t * 256, 256)])
                        for mi in range(2):
                            cc = t * 2 + mi
                            pr = pre[cc % 2]
                            pk = f"pre{cc % 2}"
                            ac = acc[cc % 2]
                            ak = f"acc{cc % 2}"

                            def ev_pre(_mi, si, ps, pkey, t0, n):
                                S.copy(pr[:, t0:t0 + n], ps, reads=[pkey], writes=[pk])
                            yield from proj_fm_g(wb, wk, [mi * 128], xT, "xT", SEGS_H, ev_pre)
                            wcol = 24 + (cbase + cc) * 4
                            S.op("dve", lambda e: e.tensor_scalar(out=ac[:], in0=pr[:, H - 3:H - 3 + T],
                                                                  scalar1=cb[:, wcol:wcol + 1], scalar2=None, op0=ALU.mult),
                                 reads=[pk, "cb"], writes=[ak])
                            for j in range(1, 4):
                                S.op("dve", lambda e: e.scalar_tensor_tensor(out=ac[:], in0=pr[:, H - 3 + j:H - 3 + j + T],
                                                                             scalar=cb[:, wcol + j:wcol + j + 1], in1=ac[:],
                                                                             op0=ALU.mult, op1=ALU.add),
                                     reads=[pk, "cb", ak], writes=[ak])
                            bcol = 56 + cbase + cc
                            if scale is None:
                                S.op("act", lambda e: e.activation(out=dst[:, cc, :], in_=ac[:], func=AF.Silu,
                                                                   bias=cb[:, bcol:bcol + 1]),
                                     reads=[ak, "cb"], writes=[dkey])
                            else:
                                S.op("act", lambda e: e.activation(out=ac[:], in_=ac[:], func=AF.Silu,
                                                                   bias=cb[:, bcol:bcol + 1]),
                                     reads=[ak, "cb"], writes=[ak])
                                S.op("dve", lambda e: e.tensor_scalar(out=dst[:, cc, :], in0=ac[:], scalar1=scale,
                                                                      scalar2=None, op0=ALU.mult),
                                     reads=[ak], writes=[dkey])

                def m1_bg_g():
                    for t in range(4):
                        wb, wk = load_w([(win_d, O_MV + t * 256, 256)])

                        def ev_mv2(ti, ps, pkey, t=t):
                            S.copy(vaug[:, ti, t, 0:256], ps, reads=[pkey], writes=["vaug"])
                        yield from proj_tm_g(wb, wk, 256, xT, "xT", [H + i * 128 for i in range(8)], ev_mv2)
                    yield from conv_proj(O_MK, kT, "kT", 4, None)
                    if prev_i is None:
                        yield from conv_proj(O_MQ, qT, "qT", 0, 128.0 ** -0.5)
                pump.set(m1_bg_g())
                pg3 = pg[:, 0:64].rearrange("p (c e) -> p c e", e=8)
                S.op("dve", lambda e: e.tensor_tensor(out=gi[:].rearrange("p (c h) -> p c h", h=4), in0=pg3[:, :, 0:4],
                                                      in1=cb[:, 16:20].unsqueeze(1).broadcast_to([128, 8, 4]), op=ALU.add),
                     reads=["pp0", "cb"], writes=["gi"])
                S.op("dve", lambda e: e.tensor_tensor(out=gf[:].rearrange("p (c h) -> p c h", h=4), in0=pg3[:, :, 4:8],
                                                      in1=cb[:, 20:24].unsqueeze(1).broadcast_to([128, 8, 4]), op=ALU.add),
                     reads=["pp0", "cb"], writes=["gf"])
                S.op("act", lambda e: e.activation(out=gf[:], in_=gf[:], func=AF.Sigmoid), reads=["gf"], writes=["gf"])
                S.op("act", lambda e: e.activation(out=gf[:], in_=gf[:], func=AF.Ln), reads=["gf"], writes=["gf"])
                S.op("pe", lambda e: e.matmul(pa[0][:, 0:32], lhsT=Umat, rhs=gf[:], start=True, stop=True),
                     reads=["mats", "gf"], writes=["pa0"])
                S.op("pe", lambda e: e.matmul(pa[0][:, 32:64], lhsT=ones, rhs=gf[:], start=True, stop=True),
                     reads=["mats", "gf"], writes=["pa0"])
                S.op("dve", lambda e: e.tensor_copy(out=bq[:], in_=pa[0][:, 0:32]), reads=["pa0"], writes=["bq"])
                S.op("dve", lambda e: e.tensor_copy(out=grep[:], in_=pa[0][:, 32:64]), reads=["pa0"], writes=["grep"])
                pump(4)
                S.op("dve", lambda e: e.tensor_tensor(out=apr[:], in0=gi[:], in1=bq[:], op=ALU.subtract),
                     reads=["gi", "bq"], writes=["apr"])
                pump(4)
                for c in range(8):
                    s2 = c % 2
                    S.op("dve", lambda e: e.tensor_tensor(out=dA[s2][:], in0=ident.unsqueeze(1).broadcast_to([128, 4, 128]),
                                                          in1=apr[:, c * 4:(c + 1) * 4].unsqueeze(2).broadcast_to([128, 4, 128]),
                                                          op=ALU.mult),
                         reads=["mats", "apr"], writes=[f"dA{s2}"])
                    S.op("pe", lambda e: e.matmul(pa[1][:, :], lhsT=ones, rhs=dA[s2][:].rearrange("p a b -> p (a b)"),
                                                  start=True, stop=True),
                         reads=["mats", f"dA{s2}"], writes=["pa1"])
                    pav = pa[1][:, :].rearrange("p (a b) -> p a b", a=4)
                    S.op("dve", lambda e: e.tensor_reduce(out=mx[:, c * 4:(c + 1) * 4], in_=pav, axis=AX.X, op=ALU.max),
                         reads=["pa1"], writes=["mx"])
                    S.op("dve", lambda e: e.tensor_tensor(out=tA[s2][:], in0=pav,
                                                          in1=cmask.unsqueeze(1).broadcast_to([128, 4, 128]), op=ALU.add),
                         reads=["pa1", "mats"], writes=[f"tA{s2}"])
                    S.op("dve", lambda e: e.tensor_reduce(out=cm[:, c * 4:(c + 1) * 4], in_=tA[s2][:], axis=AX.X, op=ALU.max),
                         reads=[f"tA{s2}"], writes=["cm"])
                    pump(6)
                S.op("dve", lambda e: e.tensor_copy(out=Rs[:, 28:32], in_=grep[:, 28:32]), reads=["grep"], writes=["Rs"])
                for c in range(6, -1, -1):
                    S.op("dve", lambda e: e.tensor_tensor(out=Rs[:, c * 4:(c + 1) * 4], in0=Rs[:, (c + 1) * 4:(c + 2) * 4],
                                                          in1=grep[:, c * 4:(c + 1) * 4], op=ALU.add),
                         reads=["Rs", "grep"], writes=["Rs"])
                S.op("dve", lambda e: e.tensor_tensor(out=tmp32[:], in0=Rs[:], in1=mx[:], op=ALU.add),
                     reads=["Rs", "mx"], writes=["tmp32"])
                S.op("dve", lambda e: e.tensor_reduce(out=mseg[:], in_=tmp32[:].rearrange("p (c h) -> p h c", h=4),
                                                      axis=AX.X, op=ALU.max), reads=["tmp32"], writes=["mseg"])
                S.op("dve", lambda e: e.tensor_tensor(out=tmp32[:], in0=apr[:], in1=Rs[:], op=ALU.add),
                     reads=["apr", "Rs"], writes=["tmp32"])
                S.op("dve", lambda e: e.tensor_tensor(out=tmp32[:].rearrange("p (c h) -> p c h", h=4),
                                                      in0=tmp32[:].rearrange("p (c h) -> p c h", h=4),
                                                      in1=mseg[:].unsqueeze(1).broadcast_to([128, 8, 4]), op=ALU.subtract),
                     reads=["tmp32", "mseg"], writes=["tmp32"])
                S.op("act", lambda e: e.activation(out=wts[:], in_=tmp32[:], func=AF.Exp), reads=["tmp32"], writes=["wts"])

                pump.drain()
                for c in range(8):
                    for hh in range(4):
                        s2 = (c * 4 + hh) % 2
                        pb = psb[:, s2 * 256:s2 * 256 + 128]
                        S.op("pe", lambda e: e.transpose(out=pb, in_=kT[:, hh, c * 128:(c + 1) * 128], identity=identb[:]),
                             reads=["kT", "identb"], writes=["psb"])
                        S.copy(ktm[:, c, hh, :], pb, reads=["psb"], writes=["ktm"])
                if prev_i is not None:
                    for c in range(8):
                        s2 = c % 2
                        S.op("dve", lambda e: e.tensor_tensor(out=wv[s2][:], in0=vaug[:, c, :, :],
                                                              in1=wts[:, c * 4:(c + 1) * 4].unsqueeze(2).broadcast_to([128, 4, 257]),
                                                              op=ALU.mult),
                             reads=["vaug", "wts"], writes=[f"wv{s2}"])
                        for hh in range(4):
                            S.op("pe", lambda e: e.matmul(pp[hh][:, 0:257], lhsT=ktm[:, c, hh, :], rhs=wv[s2][:, hh, :],
                                                          start=(c == 0), stop=(c == 7)),
                                 reads=["ktm", f"wv{s2}"], writes=[f"pp{hh}"], inc=True)
                    for hh in range(4):
                        S.copy(sums[:, prev_i, hh * 257:(hh + 1) * 257], pp[hh][:, 0:257], reads=[f"pp{hh}"], writes=[f"sums{prev_i}"])
                    S.op("dve", lambda e: e.tensor_copy(out=sums[:, prev_i, 1028:1032], in_=mseg[:]), reads=["mseg"], writes=[f"sums{prev_i}"])
                    S.op("dve", lambda e: e.tensor_copy(out=sums[:, prev_i, 1032:1036], in_=Rs[:, 0:4]), reads=["Rs"], writes=[f"sums{prev_i}"])
                S.barrier()

        def load_x(src):
            for g in range(8):
                S.dma("pool", xT[:, g * 4:(g + 1) * 4, :], src[:, g * 4:(g + 1) * 4, :],
                      writes=[f"xT{g}"], sem=f"xT{g}")


        for pi in range(3):
            load_x(xprev_d[pi])
            with ExitStack() as ph:
                m1_pass(alloc_m(ph), pi)
        load_x(xT_d)
        stop_if("P")

        S.barrier(("pe", "act", "dve", "sp", "pool"))
        ph23 = ExitStack()
        mkT = sb(ph23, "mkT", [128, 8, 256], BF16)
        mv = sb(ph23, "mv", [128, 2, 1024], BF16)
        memT = sb(ph23, "memT", [128, 32, 256], BF16)
        S.dma("pool", memT[:], memT_d, writes=["memT"], sem="memT")
        with ExitStack() as ph:
            akT = sb(ph, "akT", [128, 4, TH], BF16)
            av = sb(ph, "av", [128, 9, 512], BF16)
            bm = sb(ph, "bm", [128, 16, 256], F32)
            masks = sb(ph, "masks", [128, 2, 256], F32)
            qh = [sb(ph, f"qh{i}", [128, T], BF16) for i in range(2)]
            zh = [sb(ph, f"zh{i}", [128, T], BF16) for i in range(2)]
            ao = [sb(ph, f"ao{i}", [128, T], BF16) for i in range(2)]
            L = attn_scratch(ph)
            alloc_sg(ph, 512)
            S.dma("sp", bm[:], bias_d, writes=["bm"], sem="c_bm")
            S.dma("sp", masks[:], masks_d, writes=["masks"], sem="c_masks")
            S.op("dve", lambda e: e.tensor_tensor(out=bm[:], in0=bm[:],
                                                  in1=masks[:, 0, :].unsqueeze(1).broadcast_to([128, 16, 256]),
                                                  op=ALU.add),
                 reads=["bm", "masks"], writes=["bm"])
            for t in range(2):
                wb, wk = load_w([(win_d, O_AK + t * 256, 256)])

                def ev_k(mi, si, ps, pkey, t0, n, t=t):
                    S.copy(akT[:, t * 2 + mi, t0:t0 + n], ps, reads=[pkey], writes=["akT"])
                proj_fm(wb, wk, [0, 128], xT, "xT", SEGS_H, ev_k)
            for t in range(2):
                wb, wk = load_w([(win_d, O_AV + t * 256, 256)])

                def ev_v(ti, ps, pkey, t=t):
                    S.copy(av[:, ti, t * 256:(t + 1) * 256], ps, reads=[pkey], writes=["av"])
                proj_tm(wb, wk, 256, xT, "xT", [i * 128 for i in range(9)], ev_v)
            def ld_head(h):
                return load_w([(win_d, O_AQ + h * 128, 128), (win_d, O_AZ + h * 128, 128)])

            def head_proj_g(h, wbk):
                r = h % 2
                wb, wk = wbk

                def ev_qz(mi, si, ps, pkey, t0, n):
                    if mi == 0:
                        S.op("dve", lambda e: e.tensor_scalar(out=qh[r][:, t0 - H:t0 - H + n], in0=ps,
                                                              scalar1=128.0 ** -0.5, scalar2=None, op0=ALU.mult),
                             reads=[pkey], writes=[f"qh{r}"])
                    else:
                        gate_evac("silu", zh[r][:, t0 - H:t0 - H + n], f"zh{r}", ps, pkey, n)
                yield from proj_fm_g(wb, wk, [0, 128], xT, "xT", SEGS, ev_qz)

            run(head_proj_g(0, ld_head(0)))
            jobs = []
            for h in range(1, 16):
                jobs.append(("head", h))
                if h <= 8:
                    jobs.append(("kv", h - 1))
            jw = {}
            hdone = [1]
            limit = [0]

            def ldj(i):
                kind, idx = jobs[i]
                if kind == "head":
                    jw[i] = ld_head(idx)
                elif idx < 4:
                    jw[i] = load_w([(wkv_d, idx * 256, 256)])
                else:
                    jw[i] = load_w([(wkv_d, 1024 + (idx - 4) * 256, 256)])

            def jobs_g():
                for i, (kind, idx) in enumerate(jobs):
                    while i >= limit[0]:
                        yield
                    wb, wk = jw[i]
                    if kind == "head":
                        yield from head_proj_g(idx, jw[i])
                        hdone[0] = idx + 1
                    elif idx < 4:
                        def ev_mk(mi, si, ps, pkey, t0, n, t=idx):
                            S.copy(mkT[:, t * 2 + mi, :], ps, reads=[pkey], writes=["mkT"])
                        yield from proj_fm_g(wb, wk, [0, 128], memT, "memT", [(0, 256)], ev_mk)
                    else:
                        def ev_mv(ti, ps, pkey, t=idx - 4):
                            S.copy(mv[:, ti, t * 256:(t + 1) * 256], ps, reads=[pkey], writes=["mv"])
                        yield from proj_tm_g(wb, wk, 256, memT, "memT", [0, 128], ev_mv)
                    if i + 2 < len(jobs):
                        ldj(i + 2)

            ldj(0)
            ldj(1)
            pump.set(jobs_g())
            for h in range(16):
                g = h // 4
                r = h % 2
                lim = [i for i, (k, idx) in enumerate(jobs) if k == "head" and idx == h + 2]
                limit[0] = lim[0] if lim else len(jobs)
                while hdone[0] < h + 1:
                    pump(1)

                def stA(n):
                    slot = n % 2
                    S.op("pe", lambda e: e.matmul(pa[slot][:, 0:256], lhsT=qh[r][:, n * 128:(n + 1) * 128],
                                                  rhs=akT[:, g, n * 128:n * 128 + 256], start=True, stop=True),
                         reads=[f"qh{r}", "akT"], writes=[f"pa{slot}"])
                    sm_part1(L, slot, pa[slot][:, 0:256], f"pa{slot}", bm[:, h, :], ["bm"],
                             masks[:, 1, :] if n == 0 else None, cb[:, h:h + 1])

                stA(0)
                for n in range(8):
                    if n + 1 < 8:
                        stA(n + 1)
                    sm_part2(L, n % 2, lambda dc, hf: av[:, n + hf, g * 128:(g + 1) * 128], ["av"],
                             lambda dc: zh[r][:, n * 128:(n + 1) * 128], [f"zh{r}"],
                             lambda dc: ao[r][:, n * 128:(n + 1) * 128], [f"ao{r}"], 1, 2)
                S.dma("sp", mix_d[:, h, :], ao[r][:], reads=[f"ao{r}"], sem=f"st_ao{r}")
            limit[0] = len(jobs)
            pump.drain()
            S.barrier(("pe", "act", "dve", "sp", "pool"))
            stop_if("S2")

        with ExitStack() as ph:
            cq = [sb(ph, f"cq{i}", [128, 2, T], BF16) for i in range(2)]
            cz = [sb(ph, f"cz{i}", [128, 2, T], BF16) for i in range(2)]
            co = [sb(ph, f"co{i}", [128, 2, T], BF16) for i in range(2)]
            L = attn_scratch(ph)
            alloc_sg(ph, 512)
            jobs = []
            for h in range(4):
                jobs.append(("q", h, O_CQ + h * 256))
                jobs.append(("z", h, O_CZ + h * 256))
            jw = {}
            jdone = [0]

            def ldj(i):
                jw[i] = load_w([(win_d, jobs[i][2], 256)])

            def jobs_g():
                for i, (kind, h, col) in enumerate(jobs):
                    r = h % 2
                    wb, wk = jw[i]
                    if kind == "q":
                        def ev(mi, si, ps, pkey, t0, n, r=r):
                            S.op("dve", lambda e: e.tensor_scalar(out=cq[r][:, mi, t0 - H:t0 - H + n], in0=ps,
                                                                  scalar1=256.0 ** -0.5, scalar2=None, op0=ALU.mult),
                                 reads=[pkey], writes=[f"cq{r}"])
                    else:
                        def ev(mi, si, ps, pkey, t0, n, r=r):
                            gate_evac("silu", cz[r][:, mi, t0 - H:t0 - H + n], f"cz{r}", ps, pkey, n)
                    yield from proj_fm_g(wb, wk, [0, 128], xT, "xT", SEGS, ev)
                    jdone[0] = i + 1
                    if i + 2 < len(jobs):
                        ldj(i + 2)

            ldj(0)
            ldj(1)
            pump.set(jobs_g())
            for h in range(4):
                r = h % 2
                while jdone[0] < 2 * h + 2:
                    pump(1)

                def stA(n):
                    slot = n % 2
                    for dc in range(2):
                        S.op("pe", lambda e: e.matmul(pa[slot][:, 0:256], lhsT=cq[r][:, dc, n * 128:(n + 1) * 128],
                                                      rhs=mkT[:, h * 2 + dc, :], start=(dc == 0), stop=(dc == 1)),
                             reads=[f"cq{r}", "mkT"], writes=[f"pa{slot}"], inc=(dc == 1))
                    sm_part1(L, slot, pa[slot][:, 0:256], f"pa{slot}", None, [], None, None)

                stA(0)
                for n in range(8):
                    if n + 1 < 8:
                        stA(n + 1)
                    sm_part2(L, n % 2, lambda dc, hf: mv[:, hf, h * 256 + dc * 128:h * 256 + (dc + 1) * 128], ["mv"],
                             lambda dc: cz[r][:, dc, n * 128:(n + 1) * 128], [f"cz{r}"],
                             lambda dc: co[r][:, dc, n * 128:(n + 1) * 128], [f"co{r}"], 2, 2)
                S.dma("sp", mix_d[:, 24 + h * 2:26 + h * 2, :], co[r][:], reads=[f"co{r}"],
                      sem=f"st_co{r}")
            pump.drain()
            S.barrier()
            stop_if("S3")
        ph23.close()

        with ExitStack() as ph:
            Bf = alloc_m(ph)
            (vaug, kT, ktm, qT, gi, gf, bq, grep, apr, mx, cm, Rs, tmp32, wts, mseg) = Bf
            m1_pass(Bf, None)
            stop_if("M1")

            with ExitStack() as p2:
                Cst = sb(p2, "Cst", [128, 4, 257], F32)
                Cbf = sb(p2, "Cbf", [128, 4, 257], BF16)
                mst = sb(p2, "mst", [128, 4], F32)
                sc = [sb(p2, f"sc{i}", [128, 8, 4], F32) for i in range(2)]
                og = sb(p2, "og", [128, 8, 256], BF16)
                zg = sb(p2, "zg", [128, 8, 256], BF16)
                gvec = sb(p2, "gvec", [128, 1024], F32)
                moT = [sb(p2, "moT0", [128, 2, T], BF16)] * 2
                alloc_sg(p2, 256)
                Mv = sb(p2, "Mv", [128, 32], F32)
                mcs = sb(p2, "mcs", [128, 9, 4], F32)
                Ml = sb(p2, "Ml", [128, 32], F32)
                wint = sb(p2, "wint", [128, 32], F32)
                spv = sb(p2, "spv", [128, 32], F32)
                w2 = sb(p2, "w2", [128, 32], F32)
                emt = sb(p2, "emt", [128, 32], F32)
                dM = [sb(p2, f"dM{i}", [128, 128], F32) for i in range(2)]
                ET = [sb(p2, f"ET{i}", [128, 128], F32) for i in range(2)]
                Sm = [sb(p2, f"Sm{i}", [128, 128], BF16) for i in range(2)]
                it = [sb(p2, f"it{i}", [128, 257], F32) for i in range(2)]
                oa = [sb(p2, f"oa{i}", [128, 257], F32) for i in range(2)]
                hs = [sb(p2, f"hs{i}", [128, 256], F32) for i in range(2)]
                mo = [sb(p2, f"mo{i}", [128, 256], BF16) for i in range(2)]
                wv2 = [sb(p2, f"wv2{i}", [128, 257], BF16) for i in range(2)]
                s6 = [sb(p2, f"s6{i}", [128, 16], F32) for i in range(2)]
                S.dma("sp", gvec[:], gv_d, writes=["gvec"], sem="c_gv")
                S.op("dve", lambda e: e.memset(Cst[:].rearrange("p a b -> p (a b)"), 0.0), writes=["Cst"])
                S.op("dve", lambda e: e.memset(mst[:], 0.0), writes=["mst"])
                for rnk in range(3):
                    a2 = rnk % 2
                    ab, abk = sums[:, rnk, :], f"sums{rnk}"
                    s_, sk = sc[a2], f"sc{a2}"
                    mcol = cb[:, 64 + rnk:65 + rnk]
                    ncol = cb[:, 72 + rnk:73 + rnk]
                    S.op("dve", lambda e: e.tensor_scalar(out=s_[:, 0, :], in0=ab[:, 1032:1036], scalar1=mcol, scalar2=None,
                                                          op0=ALU.mult), reads=[abk, "cb"], writes=[sk])
                    S.op("dve", lambda e: e.tensor_scalar(out=s_[:, 1, :], in0=ab[:, 1028:1032], scalar1=mcol, scalar2=ncol,
                                                          op0=ALU.mult, op1=ALU.add), reads=[abk, "cb"], writes=[sk])
                    S.op("dve", lambda e: e.tensor_tensor(out=s_[:, 2, :], in0=s_[:, 0, :], in1=mst[:], op=ALU.add),
                         reads=[sk, "mst"], writes=[sk])
                    S.op("dve", lambda e: e.tensor_tensor(out=s_[:, 3, :], in0=s_[:, 2, :], in1=s_[:, 1, :], op=ALU.max),
                         reads=[sk], writes=[sk])
                    S.op("dve", lambda e: e.tensor_tensor(out=s_[:, 4, :], in0=s_[:, 2, :], in1=s_[:, 3, :], op=ALU.subtract),
                         reads=[sk], writes=[sk])
                    S.op("dve", lambda e: e.tensor_tensor(out=s_[:, 5, :], in0=s_[:, 1, :], in1=s_[:, 3, :], op=ALU.subtract),
                         reads=[sk], writes=[sk])
                    S.op("act", lambda e: e.activation(out=s_[:, 4:6, :], in_=s_[:, 4:6, :], func=AF.Exp), reads=[sk], writes=[sk])
                    for hh in range(4):
                        S.op("dve", lambda e: e.tensor_scalar(out=Cst[:, hh, :], in0=Cst[:, hh, :], scalar1=s_[:, 4, hh:hh + 1],
                                                              scalar2=None, op0=ALU.mult), reads=["Cst", sk], writes=["Cst"])
                        S.op("dve", lambda e: e.scalar_tensor_tensor(out=Cst[:, hh, :], in0=ab[:, hh * 257:(hh + 1) * 257],
                                                                     scalar=s_[:, 5, hh:hh + 1], in1=Cst[:, hh, :],
                                                                     op0=ALU.mult, op1=ALU.add),
                             reads=["Cst", sk, abk], writes=["Cst"])
                    S.op("dve", lambda e: e.tensor_copy(out=mst[:], in_=s_[:, 3, :]), reads=[sk], writes=["mst"])
                S.op("dve", lambda e: e.tensor_copy(out=mcs[:, 0, :], in_=mst[:]), reads=["mst"], writes=["mcs"])
                for c in range(8):
                    S.op("dve", lambda e: e.tensor_tensor(out=Ml[:, c * 4:(c + 1) * 4], in0=mcs[:, c, :],
                                                          in1=mx[:, c * 4:(c + 1) * 4], op=ALU.max),
                         reads=["mcs", "mx"], writes=["Ml"])
                    S.op("dve", lambda e: e.tensor_tensor(out=mcs[:, c + 1, :], in0=Ml[:, c * 4:(c + 1) * 4],
                                                          in1=grep[:, c * 4:(c + 1) * 4], op=ALU.add),
                         reads=["Ml", "grep"], writes=["mcs"])
                mc32 = mcs[:, 0:8, :].rearrange("p c h -> p (c h)")
                S.op("dve", lambda e: e.tensor_tensor(out=Mv[:], in0=mc32, in1=cm[:], op=ALU.max),
                     reads=["mcs", "cm"], writes=["Mv"])
                S.op("dve", lambda e: e.tensor_tensor(out=wint[:], in0=mc32, in1=Mv[:], op=ALU.subtract),
                     reads=["mcs", "Mv"], writes=["wint"])
                S.op("act", lambda e: e.activation(out=wint[:], in_=wint[:], func=AF.Exp), reads=["wint"], writes=["wint"])
                S.op("dve", lambda e: e.tensor_tensor(out=spv[:], in0=mc32, in1=Ml[:], op=ALU.subtract),
                     reads=["mcs", "Ml"], writes=["spv"])
                S.op("act", lambda e: e.activation(out=spv[:], in_=spv[:], func=AF.Exp), reads=["spv"], writes=["spv"])
                S.op("dve", lambda e: e.tensor_tensor(out=w2[:], in0=apr[:], in1=Ml[:], op=ALU.subtract),
                     reads=["apr", "Ml"], writes=["w2"])
                S.op("act", lambda e: e.activation(out=w2[:], in_=w2[:], func=AF.Exp), reads=["w2"], writes=["w2"])
                S.op("dve", lambda e: e.tensor_tensor(out=emt[:], in0=bq[:], in1=Mv[:], op=ALU.add),
                     reads=["bq", "Mv"], writes=["emt"])
                S.op("act", lambda e: e.activation(out=emt[:], in_=emt[:], func=AF.Exp, scale=-1.0), reads=["emt"], writes=["emt"])

                TOFF = [H + i * 128 for i in range(8)]
                jw2 = {}
                tdone = [0]
                blim = [8]
                bgi = [0]

                def ld2(h):
                    jw2[(h, "o")] = load_w([(win_d, O_MO + h * 256, 256)])
                    jw2[(h, "z")] = load_w([(win_d, O_MZ + h * 256, 256)])

                def bg_tile_g(wbk, t0, dst, dkey, func):
                    wb, wk = wbk
                    half = bgi[0] % 2
                    bgi[0] += 1
                    ps = pa[1][:, half * 256:(half + 1) * 256]
                    yield from mm_group_g(ps, "pa1", lambda k: xT[:, k, t0:t0 + 128], lambda k: wb[:, k, 0:256],
                                          [wk, "xT"])
                    gate_evac(func, dst, dkey, ps, "pa1", 256)
                    yield

                def bg_g():
                    for h in range(4):
                        for ti in range(8):
                            while h * 8 + ti >= blim[0]:
                                yield
                            yield from bg_tile_g(jw2[(h, "o")], TOFF[ti], og[:, ti, :], f"og{ti}", "sigmoid")
                            yield from bg_tile_g(jw2[(h, "z")], TOFF[ti], zg[:, ti, :], f"zg{ti}", "silu")
                            tdone[0] += 1
                        if h + 1 < 4:
                            ld2(h + 1)

                ld2(0)
                pump.set(bg_g())

                def fpu(hh, c):
                    i2 = c % 2
                    ci = c * 4 + hh
                    S.op("dve", lambda e: e.tensor_scalar(out=dM[i2][:], in0=ident, scalar1=Mv[:, ci:ci + 1], scalar2=None,
                                                          op0=ALU.mult), reads=["mats", "Mv"], writes=[f"dM{i2}"])
                    pm = pa[0][:, 0:128]
                    S.op("pe", lambda e: e.matmul(pm, lhsT=ones, rhs=dM[i2][:], start=True, stop=False),
                         reads=["mats", f"dM{i2}"], writes=["pa0"], inc=False)
                    S.op("pe", lambda e: e.matmul(pm, lhsT=ident, rhs=cmaskT, start=False, stop=True),
                         reads=["mats"], writes=["pa0"])
                    S.op("act", lambda e: e.activation(out=ET[i2][:], in_=pm, func=AF.Exp, bias=apr[:, ci:ci + 1], scale=-1.0),
                         reads=["pa0", "apr"], writes=[f"ET{i2}"])
                    pump(1)
                    pst = pa[0][:, 128:256]
                    S.op("pe", lambda e: e.matmul(pst, lhsT=kT[:, hh, c * 128:(c + 1) * 128], rhs=qT[:, hh, c * 128:(c + 1) * 128],
                                                  start=True, stop=True), reads=["kT", "qT"], writes=["pa0"])
                    S.op("dve", lambda e: e.tensor_tensor(out=Sm[i2][:], in0=pst, in1=ET[i2][:], op=ALU.mult),
                         reads=["pa0", f"ET{i2}"], writes=[f"Sm{i2}"])
                    pump(1)
                    pin = pp[i2 * 2]
                    pink = f"pp{i2 * 2}"
                    pit = pp[i2 * 2 + 1]
                    pitk = f"pp{i2 * 2 + 1}"
                    S.op("pe", lambda e: e.matmul(pin[:, 0:257], lhsT=Sm[i2][:], rhs=vaug[:, c, hh, :], start=True, stop=True),
                         reads=[f"Sm{i2}", "vaug"], writes=[pink])
                    S.op("pe", lambda e: e.matmul(pit[:, 0:257], lhsT=qT[:, hh, c * 128:(c + 1) * 128], rhs=Cbf[:, hh, :],
                                                  start=True, stop=True), reads=["qT", "Cbf"], writes=[pitk])
                    if c < 7:
                        S.op("dve", lambda e: e.tensor_scalar(out=wv2[i2][:], in0=vaug[:, c, hh, :], scalar1=w2[:, ci:ci + 1],
                                                              scalar2=None, op0=ALU.mult),
                             reads=["vaug", "w2"], writes=[f"wv2{i2}"])
                        pump(1)
                        pdc = po[:, 0:257]
                        S.op("pe", lambda e: e.matmul(pdc, lhsT=ktm[:, c, hh, :], rhs=wv2[i2][:], start=True, stop=True),
                             reads=["ktm", f"wv2{i2}"], writes=["po"])
                        S.op("dve", lambda e: e.scalar_tensor_tensor(out=Cst[:, hh, :], in0=Cst[:, hh, :],
                                                                     scalar=spv[:, ci:ci + 1], in1=pdc,
                                                                     op0=ALU.mult, op1=ALU.add),
                             reads=["Cst", "spv", "po"], writes=["Cst"])
                        S.op("act", lambda e: e.copy(out=Cbf[:, hh, :], in_=Cst[:, hh, :]), reads=["Cst"], writes=["Cbf"])

                def back(hh, c):
                    r = hh % 2
                    i2 = c % 2
                    ci = c * 4 + hh
                    pin = pp[i2 * 2]
                    pink = f"pp{i2 * 2}"
                    pit = pp[i2 * 2 + 1]
                    pitk = f"pp{i2 * 2 + 1}"
                    S.op("act", lambda e: e.mul(out=it[i2][:], in_=pit[:, 0:257], mul=wint[:, ci:ci + 1]),
                         reads=[pitk, "wint"], writes=[f"it{i2}"])
                    yield
                    S.op("dve", lambda e: e.tensor_tensor(out=oa[i2][:], in0=pin[:, 0:257], in1=it[i2][:], op=ALU.add),
                         reads=[pink, f"it{i2}"], writes=[f"oa{i2}"])
                    yield
                    pump(1)
                    q6 = s6[i2]
                    qk = f"s6{i2}"
                    S.op("act", lambda e: e.activation(out=q6[:, 0:1], in_=oa[i2][:, 256:257], func=AF.Abs),
                         reads=[f"oa{i2}"], writes=[qk])
                    yield
                    S.op("dve", lambda e: e.tensor_tensor(out=q6[:, 1:2], in0=q6[:, 0:1], in1=emt[:, ci:ci + 1], op=ALU.max),
                         reads=[qk, "emt"], writes=[qk])
                    yield
                    S.op("dve", lambda e: e.reciprocal(out=q6[:, 2:3], in_=q6[:, 1:2]), reads=[qk], writes=[qk])
                    yield
                    pump(1)
                    S.op("dve", lambda e: e.scalar_tensor_tensor(out=hs[i2][:], in0=oa[i2][:, 0:256], scalar=q6[:, 2:3],
                                                                 in1=og[:, c, :], op0=ALU.mult, op1=ALU.mult),
                         reads=[f"oa{i2}", qk, f"og{c}"], writes=[f"hs{i2}"])
                    yield
                    S.op("dve", lambda e: e.bn_stats(out=q6[:, 4:10], in_=hs[i2][:]), reads=[f"hs{i2}"], writes=[qk])
                    yield
                    S.op("dve", lambda e: e.bn_aggr(out=q6[:, 10:12], in_=q6[:, 4:10]), reads=[qk], writes=[qk])
                    yield
                    pump(1)
                    S.op("act", lambda e: e.activation(out=q6[:, 12:13], in_=q6[:, 11:12], func=AF.Ln, bias=cb[:, 81:82]),
                         reads=[qk, "cb"], writes=[qk])
                    yield
                    S.op("act", lambda e: e.activation(out=q6[:, 13:14], in_=q6[:, 12:13], func=AF.Exp, scale=-0.5),
                         reads=[qk], writes=[qk])
                    yield
                    pump(1)
                    S.op("dve", lambda e: e.tensor_scalar(out=hs[i2][:], in0=hs[i2][:], scalar1=q6[:, 10:11],
                                                          scalar2=q6[:, 13:14], op0=ALU.subtract, op1=ALU.mult),
                         reads=[f"hs{i2}", qk], writes=[f"hs{i2}"])
                    yield
                    S.op("dve", lambda e: e.tensor_tensor(out=hs[i2][:], in0=hs[i2][:], in1=gvec[:, hh * 256:(hh + 1) * 256],
                                                          op=ALU.mult), reads=[f"hs{i2}", "gvec"], writes=[f"hs{i2}"])
                    yield
                    pump(1)
                    S.op("dve", lambda e: e.tensor_tensor(out=mo[i2][:], in0=hs[i2][:], in1=zg[:, c, :], op=ALU.mult),
                         reads=[f"hs{i2}", f"zg{c}"], writes=[f"mo{i2}"])
                    yield
                    pump(2)
                    for fc in range(2):
                        pb = psb[:, i2 * 256 + fc * 128:i2 * 256 + (fc + 1) * 128]
                        S.op("pe", lambda e: e.transpose(out=pb, in_=mo[i2][:, fc * 128:(fc + 1) * 128], identity=identb[:]),
                             reads=[f"mo{i2}", "identb"], writes=["psb"])
                    yield
                    S.op("act", lambda e: e.copy(out=moT[r][:, :, c * 128:(c + 1) * 128],
                                                 in_=psb[:, i2 * 256:(i2 + 1) * 256].rearrange("p (a b) -> p a b", a=2)),
                         reads=["psb"], writes=["moT"])
                    yield
                    pump(1)

                for hh in range(4):
                    r = hh % 2
                    while tdone[0] < hh * 8 + 1:
                        pump(1)
                    S.op("dve", lambda e: e.tensor_copy(out=Cbf[:, hh, :], in_=Cst[:, hh, :]), reads=["Cst"], writes=["Cbf"])
                    for c in range(0, 8, 2):
                        blim[0] = hh * 8 + c + 8
                        fpu(hh, c)
                        fpu(hh, c + 1)
                        while tdone[0] < min(32, hh * 8 + c + 2):
                            pump(1)
                        g0 = back(hh, c)
                        g1 = back(hh, c + 1)
                        alive = [g0, g1]
                        while alive:
                            for gg in list(alive):
                                try:
                                    next(gg)
                                except StopIteration:
                                    alive.remove(gg)
                    S.dma("sp", mix_d[:, 16 + hh * 2:18 + hh * 2, :], moT[r][:], reads=["moT"],
                          sem="st_mo")
                blim[0] = 64
                pump.drain()
                S.barrier()
                stop_if("M2")
            S.barrier()

        with ExitStack() as ph:
            mixT = xT[:].rearrange("p k t -> p (k t)")[:, 0:32 * T].rearrange("p (k t) -> p k t", k=32)
            xr = [sb(ph, f"xr{i}", [128, 256], F32) for i in range(4)]
            zt = [sb(ph, f"zt{i}", [128, 2048], F32) for i in range(4)]
            lng = sb(ph, "lng", [128, 4096], F32)
            lnb = sb(ph, "lnb", [128, 4096], F32)
            stt = sb(ph, "stt", [128, 8, 96], F32)
            mvs = sb(ph, "mvs", [128, 8, 2], F32)
            rs = sb(ph, "rs", [128, 8, 4], F32)
            S.dma("sp", lng[:], lng_d, writes=["lng"], sem="c_lng")
            S.dma("sp", lnb[:], lnb_d, writes=["lnb"], sem="c_lnb")
            for g in range(4):
                S.dma("sp", mixT[:, g * 8:(g + 1) * 8, :], mix_d[:, g * 8:(g + 1) * 8, :], writes=[f"xT{2 * g}", f"xT{2 * g + 1}"],
                      sem=f"mixl{g}")
            alpha = 2.0 ** 0.25
            xi = 0
            for t in range(16):
                wb, wk = load_w([(wout_d, t * 256, 256)])
                for c in range(8):
                    x4 = xi % 4
                    xi += 1
                    S.dma("sp", xr[x4][:], xres_d[c * 128:(c + 1) * 128, t * 256:(t + 1) * 256], writes=[f"xr{x4}"],
                          sem=f"xr{x4}")
                    ps, pkey = next_pp()
                    mm_group(ps[:, 0:256], pkey, lambda k: mixT[:, k, c * 128:(c + 1) * 128], lambda k: wb[:, k, :],
                             [wk, "xT"])
                    S.op("dve", lambda e: e.scalar_tensor_tensor(out=xr[x4][:], in0=xr[x4][:], scalar=alpha, in1=ps[:, 0:256],
                                                                 op0=ALU.mult, op1=ALU.add),
                         reads=[f"xr{x4}", pkey], writes=[f"xr{x4}"])
                    S.op("dve", lambda e: e.bn_stats(out=stt[:, c, t * 6:(t + 1) * 6], in_=xr[x4][:]),
                         reads=[f"xr{x4}"], writes=["stt"])
                    S.dma("act", zscr_d[c * 128:(c + 1) * 128, t * 256:(t + 1) * 256], xr[x4][:], reads=[f"xr{x4}"],
                          sem=f"zs{x4}")
            for c in range(8):
                S.op("dve", lambda e: e.bn_aggr(out=mvs[:, c, :], in_=stt[:, c, :]), reads=["stt"], writes=["mvs"])
            S.op("act", lambda e: e.activation(out=rs[:, :, 0], in_=mvs[:, :, 1], func=AF.Sqrt, bias=cb[:, 80:81]),
                 reads=["mvs", "cb"], writes=["rs"])
            S.op("dve", lambda e: e.reciprocal(out=rs[:, :, 1], in_=rs[:, :, 0]), reads=["rs"], writes=["rs"])
            S.op("dve", lambda e: e.scalar_tensor_tensor(out=rs[:, :, 2], in0=mvs[:, :, 0], scalar=-1.0, in1=rs[:, :, 1],
                                                         op0=ALU.mult, op1=ALU.mult), reads=["mvs", "rs"], writes=["rs"])
            S.barrier()
            for i in range(16):
                c, hf = i // 2, i % 2
                z4 = i % 4
                z, zk = zt[z4], f"zt{z4}"
                cs = slice(hf * 2048, (hf + 1) * 2048)
                S.dma("sp", z[:], zscr_d[c * 128:(c + 1) * 128, cs], writes=[zk], sem=f"zl{z4}")
                S.op("act", lambda e: e.activation(out=z[:], in_=z[:], func=AF.Identity, bias=rs[:, c, 2:3], scale=rs[:, c, 1:2]),
                     reads=[zk, "rs"], writes=[zk])
                S.op("dve", lambda e: e.tensor_tensor(out=z[:], in0=z[:], in1=lng[:, cs], op=ALU.mult),
                     reads=[zk, "lng"], writes=[zk])
                if i % 3 == 2:
                    S.op("dve", lambda e: e.tensor_tensor(out=z[:], in0=z[:], in1=lnb[:, cs], op=ALU.add),
                         reads=[zk, "lnb"], writes=[zk])
                    S.dma("act", out_d[c * 128:(c + 1) * 128, cs], z[:], reads=[zk], sem=f"zf{z4}")
                else:
                    S.op("pool", lambda e: e.tensor_tensor(out=z[:], in0=z[:], in1=lnb[:, cs], op=ALU.add),
                         reads=[zk, "lnb"], writes=[zk])
                    S.dma("pool", out_d[c * 128:(c + 1) * 128, cs], z[:], reads=[zk], sem=f"zf{z4}")
            S.finish("sp")
    except _Stop:
        pass
    return nc


_T5 = None


def _t5_bucket(dist):
    max_exact = 16
    is_small = dist < max_exact
    ratio = np.maximum(dist, max_exact).astype(np.float32) / np.float32(max_exact)
    large = max_exact + (np.log(ratio) / np.float32(np.log(128 / max_exact)) * np.float32(16)).astype(np.int32)
    large = np.minimum(large, 31)
    return np.where(is_small, dist, large)


def _host_prep(inputs):
    f32 = np.float32
    x = np.asarray(inputs["x"], f32)
    mem = np.asarray(inputs["mem"], f32)
    w_in = np.asarray(inputs["w_in"], f32)
    w_kv = np.asarray(inputs["w_mem_kv"], f32)
    w_out = np.asarray(inputs["w_out"], f32)
    w_in_t = np.ascontiguousarray(w_in.reshape(32, 128, IN_W).transpose(1, 0, 2))
    wg_h = np.ascontiguousarray(w_in_t[:, :, O_MI:O_MI + 8])
    w_kv_t = np.ascontiguousarray(w_kv.reshape(32, 128, 2048).transpose(1, 0, 2))
    w_out_t = np.ascontiguousarray(w_out.reshape(32, 128, 4096).transpose(1, 0, 2))
    rep = lambda v: np.ascontiguousarray(np.broadcast_to(np.asarray(v, f32)[None, :], (128, len(v))))
    gvec = rep(inputs["m_norm_g"])
    lng = rep(inputs["ln_g"])
    lnb = rep(inputs["ln_b"])
    qi = np.arange(128, dtype=np.int32)[:, None]
    si = np.arange(256, dtype=np.int32)[None, :]
    dist = qi + 128 - si
    bucket = _t5_bucket(np.clip(dist, 0, 127))
    rel_bias = np.asarray(inputs["rel_bias"], f32)
    bias = np.ascontiguousarray(rel_bias[bucket].transpose(0, 2, 1))
    winmask = np.where((dist >= 0) & (dist < 128), 0.0, NEG).astype(f32)
    mats = np.zeros((128, 5, 128), f32)
    mats[:, 0] = np.eye(128)
    mats[:, 1] = 1.0
    ii = np.arange(128)
    mats[:, 2] = (ii[:, None] <= ii[None, :])
    mats[:, 3] = np.where(ii[None, :] <= ii[:, None], 0.0, NEG)
    mats[:, 4] = np.where(ii[:, None] <= ii[None, :], 0.0, -NEG)
    conv_w = np.asarray(inputs["conv_w"], f32)
    conv_b = np.asarray(inputs["conv_b"], f32)
    in_maps = []
    for c in range(8):
        b, j = c // 4, c % 4
        t0 = j * T
        seg = np.zeros((TH, 4096), f32)
        if j > 0:
            seg[:] = x[b, t0 - H:t0 + T]
        else:
            seg[H:] = x[b, 0:T]
        xT = np.ascontiguousarray(seg.T.reshape(32, 128, TH).transpose(1, 0, 2))
        xres = np.ascontiguousarray(x[b, t0:t0 + T])
        memT = np.ascontiguousarray(mem[b].T.reshape(32, 128, 256).transpose(1, 0, 2))
        cb = np.zeros((128, NCB), f32)
        cb[:, 0:16] = np.asarray(inputs["sinks"], f32)[None, :]
        cb[:, 16:20] = np.asarray(inputs["b_i"], f32)[None, :]
        cb[:, 20:24] = np.asarray(inputs["b_f"], f32)[None, :]
        cb[:, 24:56] = conv_w.reshape(4, 8, 128).transpose(2, 1, 0).reshape(128, 32)
        cb[:, 56:64] = conv_b.reshape(8, 128).T
        xprev = np.zeros((3, 128, 32, TH), f32)
        for r in range(3):
            g = j - 3 + r
            on = g >= 0
            cb[:, 64 + r] = 1.0 if on else 0.0
            cb[:, 72 + r] = 0.0 if on else NEG
            if on:
                sg = np.zeros((TH, 4096), f32)
                g0 = g * T
                if g > 0:
                    sg[:] = x[b, g0 - H:g0 + T]
                else:
                    sg[H:] = x[b, 0:T]
                xprev[r] = sg.T.reshape(32, 128, TH).transpose(1, 0, 2)
        cb[:, 80] = 1e-5
        cb[:, 81] = 1e-6
        masks = np.zeros((128, 2, 256), f32)
        masks[:, 0] = winmask
        if j == 0:
            masks[:, 1, 0:128] = NEG
        in_maps.append({"xT": xT, "xres": xres, "xprev": xprev, "memT": memT, "w_in_t": w_in_t, "w_kv_t": w_kv_t,
                        "w_out_t": w_out_t, "cb": cb, "gvec": gvec, "lng": lng, "lnb": lnb,
                        "bias": bias, "mats": mats, "masks": masks, "wg": wg_h})
    return in_maps


_NC = None


def kernel(**inputs):
    global _NC
    in_maps = _host_prep(inputs)
    if _NC is None:
        _NC = build_program()
    res = run_bass_kernel_spmd(_NC, in_maps, core_ids=list(range(8)))
    out = np.empty((2, 4096, 4096), np.float32)
    for c in range(8):
        b, j = c // 4, c % 4
        out[b, j * T:(j + 1) * T] = np.asarray(res.results[c]["out"], np.float32)
    kernel.last_results = res.results
    return out
```

```python
import numpy as np
from contextlib import ExitStack
import concourse.bass as bass
import concourse.mybir as mybir
from concourse.bass_utils import run_bass_kernel_spmd

F32 = mybir.dt.float32
BF16 = mybir.dt.bfloat16
AF = mybir.ActivationFunctionType
ALU = mybir.AluOpType
AX = mybir.AxisListType

DEBUG = False
STOP = None


class _Stop(Exception):
    pass
T = 1024
H = 128
TH = T + H
NEG = -1.0e30
NCB = 96
O_AQ, O_AK, O_AV, O_AZ = 0, 2048, 2560, 3072
O_MQ, O_MK, O_MV, O_MI, O_MO, O_MZ = 5120, 5632, 6144, 7168, 7176, 8200
O_CQ, O_CZ = 9224, 10248
IN_W = 11272


class Sched:
    def __init__(self, nc, es):
        self.nc = nc
        self.es = es
        self.engs = {"pe": nc.tensor, "act": nc.scalar, "dve": nc.vector,
                     "pool": nc.gpsimd, "sp": nc.sync}
        self.esem = {}
        self.ecnt = {}
        for n in self.engs:
            self.esem[n] = es.enter_context(nc.semaphore("s_" + n))
            self.ecnt[n] = 0
        self.seen = {n: {} for n in self.engs}
        self.state = {}
        self.dsem = {}
        self.dcnt = {}
        self.flip = 0

    def _tok_wait(self, en, tok):
        if tok is None:
            return
        sem, val = tok
        if en == "pe" and sem is self.esem["pe"]:
            return
        k = id(sem)
        if self.seen[en].get(k, 0) >= val:
            return
        self.engs[en].wait_ge(sem, val)
        self.seen[en][k] = val

    def _deps(self, en, reads, writes, skip_sem=None):
        for k in reads:
            st = self.state.get(k)
            if st:
                self._tok_wait(en, st[0])
        for k in writes:
            st = self.state.get(k)
            if st:
                if not (skip_sem is not None and st[0] is not None and st[0][0] is skip_sem):
                    self._tok_wait(en, st[0])
                for t in st[1].values():
                    self._tok_wait(en, t)

    def _commit(self, tok, reads, writes):
        key = id(tok[0])
        for k in reads:
            st = self.state.setdefault(k, [None, {}])
            if key not in st[1] or st[1][key][1] < tok[1]:
                st[1][key] = tok
        for k in writes:
            self.state[k] = [tok, {}]

    PSK = {"pp0", "pp1", "pp2", "pp3", "pa0", "pa1", "psb", "po"}

    def op(self, en, fn, reads=(), writes=(), inc=True):
        pr = [k for k in reads if k in self.PSK]
        if pr:
            reads = [k for k in reads if k not in self.PSK]
            writes = list(writes) + [k for k in pr if k not in writes]
        self._deps(en, reads, writes)
        ins = fn(self.engs[en])
        if inc:
            self.ecnt[en] += 1
            ins.then_inc(self.esem[en], 1)
            tok = (self.esem[en], self.ecnt[en])
        else:
            tok = (self.esem[en], self.ecnt[en] + 1)
        self._commit(tok, reads, writes)
        return ins

    def any(self, fn, reads=(), writes=()):
        self.flip ^= 1
        return self.op("act" if self.flip else "dve", fn, reads, writes)

    def copy(self, out, in_, reads=(), writes=()):
        self.flip ^= 1
        if self.flip:
            return self.op("act", lambda e: e.copy(out=out, in_=in_), reads, writes)
        return self.op("dve", lambda e: e.tensor_copy(out=out, in_=in_), reads, writes)

    def getsem(self, sem):
        if sem not in self.dsem:
            self.dsem[sem] = self.es.enter_context(self.nc.semaphore("d_" + sem))
            self.dcnt[sem] = 0
        return self.dsem[sem]

    def dma(self, q, out, in_, reads=(), writes=(), sem=None):
        h = self.getsem(sem)
        self._deps(q, reads, writes, skip_sem=h)
        ins = self.engs[q].dma_start(out=out, in_=in_)
        self.dcnt[sem] += 16
        ins.then_inc(h, 16)
        tok = (h, self.dcnt[sem])
        self._commit(tok, reads, writes)
        return ins

    def barrier(self, engines=("pe", "act", "dve", "sp")):
        for e in engines:
            for o in engines:
                if o != e and self.ecnt[o] > 0:
                    self._tok_wait(e, (self.esem[o], self.ecnt[o]))
            for sem, h in self.dsem.items():
                if self.dcnt[sem] > 0:
                    self._tok_wait(e, (h, self.dcnt[sem]))

    def finish(self, en="sp"):
        for sem, h in self.dsem.items():
            self._tok_wait(en, (h, self.dcnt[sem]))
        for n in self.engs:
            if n != en and self.ecnt[n] > 0:
                self._tok_wait(en, (self.esem[n], self.ecnt[n]))


def build_program():
    nc = bass.Bass("TRN2", target_bir_lowering=False)

    def din(name, shape, dt=F32):
        return nc.dram_tensor(name, shape, dt, kind="ExternalInput").ap()

    xT_d = din("xT", [128, 32, TH])
    xres_d = din("xres", [T, 4096])
    xprev_d = din("xprev", [3, 128, 32, TH])
    memT_d = din("memT", [128, 32, 256])
    win_d = din("w_in_t", [128, 32, IN_W])
    wkv_d = din("w_kv_t", [128, 32, 2048])
    wout_d = din("w_out_t", [128, 32, 4096])
    cb_d = din("cb", [128, NCB])
    wg_d = din("wg", [128, 32, 8])
    gv_d = din("gvec", [128, 1024])
    lng_d = din("lng", [128, 4096])
    lnb_d = din("lnb", [128, 4096])
    bias_d = din("bias", [128, 16, 256])
    mats_d = din("mats", [128, 5, 128])
    masks_d = din("masks", [128, 2, 256])
    out_d = nc.dram_tensor("out", [T, 4096], F32, kind="ExternalOutput").ap()
    mix_d = nc.dram_tensor("mixT_s", [128, 32, T], BF16,
                           kind="ExternalOutput" if DEBUG else "Internal").ap()
    zscr_d = nc.dram_tensor("zscr", [T, 4096], F32).ap()

    try:
      with ExitStack() as es:
        S = Sched(nc, es)

        def stop_if(tag):
            if STOP == tag:
                S.barrier(("pe", "act", "dve", "sp", "pool"))
                S.finish("sp")
                raise _Stop()

        uid = [0]

        def sb(stack, name, shape, dt):
            uid[0] += 1
            return stack.enter_context(nc.sbuf_tensor(f"sb{uid[0]}_{name}", shape, dt))

        xT = sb(es, "xT", [128, 32, TH], BF16)
        NW = 2
        wbufs = [sb(es, f"wb{i}", [128, 32, 256], BF16) for i in range(NW)]
        wg = sb(es, "wg", [128, 32, 8], BF16)
        cb = sb(es, "cb", [128, NCB], F32)
        mats = sb(es, "mats", [128, 5, 128], F32)
        identb = sb(es, "identb", [128, 128], BF16)
        ident = mats[:, 0, :]
        ones = mats[:, 1, :]
        Umat = mats[:, 2, :]
        cmask = mats[:, 3, :]
        cmaskT = mats[:, 4, :]
        pp = [es.enter_context(nc.psum_tensor(f"pp{i}", [128, 512], F32)) for i in range(4)]
        pa = [es.enter_context(nc.psum_tensor(f"pa{i}", [128, 512], F32)) for i in range(2)]
        psb = es.enter_context(nc.psum_tensor("psb", [128, 1024], BF16))
        po = es.enter_context(nc.psum_tensor("po", [128, 512], F32))
        ppi = [0]

        def next_pp():
            i = ppi[0] % 4
            ppi[0] += 1
            return pp[i], f"pp{i}"

        wli = [0]

        def load_w(parts):
            slot = wli[0] % NW
            wli[0] += 1
            wb = wbufs[slot]
            off = 0
            for (src, c0, n) in parts:
                for k0 in range(0, 32, 8):
                    S.dma("pool", wb[:, k0:k0 + 8, off:off + n], src[:, k0:k0 + 8, c0:c0 + n],
                          writes=[f"w{slot}"], sem=f"w{slot}")
                off += n
            return wb, f"w{slot}"

        def mm_group_g(ps_ap, pkey, lhs_fn, rhs_fn, reads, step=8):
            for k in range(32):
                rk = [(f"xT{k // 4}" if r_ == "xT" else r_) for r_ in reads]
                S.op("pe", lambda e: e.matmul(ps_ap, lhsT=lhs_fn(k), rhs=rhs_fn(k),
                                              start=(k == 0), stop=(k == 31)),
                     reads=rk, writes=[pkey], inc=(k == 31))
                if k % step == step - 1 and k != 31:
                    yield

        def run(g):
            for _ in g:
                pass

        def mm_group(ps_ap, pkey, lhs_fn, rhs_fn, reads):
            run(mm_group_g(ps_ap, pkey, lhs_fn, rhs_fn, reads))

        def proj_fm_g(wb, wkey, col_offs, src, srckey, segs, evac):
            for mi, off in enumerate(col_offs):
                for si, (t0, n) in enumerate(segs):
                    ps, pkey = next_pp()
                    yield from mm_group_g(ps[:, 0:n], pkey, lambda k: wb[:, k, off:off + 128],
                                          lambda k: src[:, k, t0:t0 + n], [wkey, srckey])
                    evac(mi, si, ps[:, 0:n], pkey, t0, n)
                    yield

        def proj_fm(*a):
            run(proj_fm_g(*a))

        def proj_tm_g(wb, wkey, ncols, src, srckey, toffs, evac):
            for ti, t0 in enumerate(toffs):
                ps, pkey = next_pp()
                yield from mm_group_g(ps[:, 0:ncols], pkey, lambda k: src[:, k, t0:t0 + 128],
                                      lambda k: wb[:, k, 0:ncols], [wkey, srckey])
                evac(ti, ps[:, 0:ncols], pkey)
                yield

        def proj_tm(*a):
            run(proj_tm_g(*a))

        sg_tmp = {}

        def gate_evac(kind, dst, dkey, ps, pkey, n):
            i = sg_tmp["i"] = (sg_tmp.get("i", 0) + 1) % 2
            e_, x_ = sg_tmp["e"][i], sg_tmp["x"][i]
            ke, kx = f"sge{i}", f"sgx{i}"
            S.op("act", lambda e: e.activation(out=e_[:, 0:n], in_=ps, func=AF.Exp), reads=[pkey], writes=[ke])
            S.op("act", lambda e: e.copy(out=x_[:, 0:n], in_=ps), reads=[pkey], writes=[kx])
            S.op("act", lambda e: e.activation(out=e_[:, 0:n], in_=e_[:, 0:n], func=AF.Ln, bias=mats[:, 1, 0:1]),
                 reads=[ke, "mats"], writes=[ke])
            S.op("dve", lambda e: e.tensor_tensor(out=e_[:, 0:n], in0=x_[:, 0:n], in1=e_[:, 0:n], op=ALU.subtract),
                 reads=[ke, kx], writes=[ke])
            if kind == "silu":
                S.op("act", lambda e: e.activation(out=e_[:, 0:n], in_=e_[:, 0:n], func=AF.Exp), reads=[ke], writes=[ke])
                S.op("dve", lambda e: e.tensor_tensor(out=dst, in0=e_[:, 0:n], in1=x_[:, 0:n], op=ALU.mult),
                     reads=[ke, kx], writes=[dkey])
            else:
                S.op("act", lambda e: e.activation(out=dst, in_=e_[:, 0:n], func=AF.Exp), reads=[ke], writes=[dkey])

        def alloc_sg(stack, n):
            sg_tmp["e"] = [sb(stack, f"sge{i}", [128, n], F32) for i in range(2)]
            sg_tmp["x"] = [sb(stack, f"sgx{i}", [128, n], F32) for i in range(2)]

        class Pump:
            def __init__(self):
                self.g = None

            def set(self, g):
                self.g = g

            def __call__(self, n=1):
                for _ in range(n):
                    if self.g is None:
                        return
                    try:
                        next(self.g)
                    except StopIteration:
                        self.g = None
                        return

            def drain(self):
                while self.g is not None:
                    self(1)

        pump = Pump()

        SEGS_H = [(0, 512), (512, 512), (1024, 128)]
        SEGS = [(H, 512), (H + 512, 512)]

        S.dma("sp", cb[:], cb_d, writes=["cb"], sem="c_cb")
        S.dma("sp", mats[:], mats_d, writes=["mats"], sem="c_mats")
        S.dma("pool", wg[:], wg_d, writes=["wg"], sem="wg")
        S.op("dve", lambda e: e.tensor_copy(out=identb[:], in_=ident), reads=["mats"], writes=["identb"])
        stop_if("S0")

        def softmax_pv(L, slot, ps_s, pskey, bias_ap, bias_keys, extra_ap, sink_col, v_fn, vkeys,
                       gate_fn, gkeys, out_fn, okeys, n_dc):
            sm_part1(L, slot, ps_s, pskey, bias_ap, bias_keys, extra_ap, sink_col)
            sm_part2(L, slot, v_fn, vkeys, gate_fn, gkeys, out_fn, okeys, n_dc, 0)

        def sm_part1(L, slot, ps_s, pskey, bias_ap, bias_keys, extra_ap, sink_col):
            Sb, Pn, PT, st = L["Sb"][slot], L["Pn"][slot], L["PT"][slot], L["st"][slot]
            kS, kP, kT_, kst = f"Sb{slot}", f"Pn{slot}", f"PT{slot}", f"st{slot}"
            if bias_ap is not None:
                S.op("dve", lambda e: e.tensor_tensor(out=Sb[:], in0=ps_s, in1=bias_ap, op=ALU.add),
                     reads=[pskey] + bias_keys, writes=[kS])
                if extra_ap is not None:
                    S.op("dve", lambda e: e.tensor_tensor(out=Sb[:], in0=Sb[:], in1=extra_ap, op=ALU.add),
                         reads=[kS, "masks"], writes=[kS])
                src, srck = Sb[:], kS
            else:
                src, srck = ps_s, pskey
            S.op("dve", lambda e: e.tensor_reduce(out=st[:, 0:1], in_=src, axis=AX.X, op=ALU.max),
                 reads=[srck], writes=[kst])
            if sink_col is not None:
                S.op("dve", lambda e: e.tensor_scalar(out=st[:, 1:2], in0=st[:, 0:1], scalar1=sink_col,
                                                      scalar2=-1.0, op0=ALU.max, op1=ALU.mult),
                     reads=[kst, "cb"], writes=[kst])
            else:
                S.op("dve", lambda e: e.tensor_scalar(out=st[:, 1:2], in0=st[:, 0:1], scalar1=-1.0,
                                                      scalar2=None, op0=ALU.mult),
                     reads=[kst], writes=[kst])
            S.op("act", lambda e: e.activation(out=Sb[:], in_=src, func=AF.Exp, bias=st[:, 1:2],
                                               accum_out=st[:, 2:3]),
                 reads=[srck, kst], writes=[kS, kst])
            if sink_col is not None:
                S.op("act", lambda e: e.activation(out=st[:, 3:4], in_=sink_col, func=AF.Exp, bias=st[:, 1:2]),
                     reads=[kst, "cb"], writes=[kst])
                S.op("dve", lambda e: e.tensor_tensor(out=st[:, 2:3], in0=st[:, 2:3], in1=st[:, 3:4], op=ALU.add),
                     reads=[kst], writes=[kst])
            S.op("dve", lambda e: e.reciprocal(out=st[:, 5:6], in_=st[:, 2:3]), reads=[kst], writes=[kst])
            S.op("dve", lambda e: e.tensor_scalar(out=Pn[:], in0=Sb[:], scalar1=st[:, 5:6], scalar2=None,
                                                  op0=ALU.mult),
                 reads=[kS, kst], writes=[kP])

        def sm_part2(L, slot, v_fn, vkeys, gate_fn, gkeys, out_fn, okeys, n_dc, npump):
            Sb, Pn, PT, st = L["Sb"][slot], L["Pn"][slot], L["PT"][slot], L["st"][slot]
            kS, kP, kT_, kst = f"Sb{slot}", f"Pn{slot}", f"PT{slot}", f"st{slot}"
            pump(npump)
            pb = psb[:, slot * 256:(slot + 1) * 256]
            pbk = "psb"
            for hf in range(2):
                S.op("pe", lambda e: e.transpose(out=pb[:, hf * 128:(hf + 1) * 128],
                                                 in_=Pn[:, hf * 128:(hf + 1) * 128], identity=identb[:]),
                     reads=[kP, "identb"], writes=[pbk])
            S.copy(PT[:], pb, reads=[pbk], writes=[kT_])
            pump(npump)
            for dc in range(n_dc):
                oslot = (slot * 2 + dc) % 4
                pov = po[:, oslot * 128:(oslot + 1) * 128]
                pok = "po"
                for hf in range(2):
                    S.op("pe", lambda e: e.matmul(pov, lhsT=v_fn(dc, hf), rhs=PT[:, hf * 128:(hf + 1) * 128],
                                                  start=(hf == 0), stop=(hf == 1)),
                         reads=[kT_] + vkeys, writes=[pok], inc=(hf == 1))
                S.op("dve", lambda e: e.tensor_tensor(out=out_fn(dc), in0=pov, in1=gate_fn(dc), op=ALU.mult),
                     reads=[pok] + gkeys, writes=okeys)

        def attn_scratch(stack):
            return {
                "Sb": [sb(stack, f"Sb{i}", [128, 256], F32) for i in range(2)],
                "Pn": [sb(stack, f"Pn{i}", [128, 256], BF16) for i in range(2)],
                "PT": [sb(stack, f"PT{i}", [128, 256], BF16) for i in range(2)],
                "st": [sb(stack, f"st{i}", [128, 8], F32) for i in range(2)],
            }

        sums = sb(es, "sums", [128, 3, 1036], F32)

        def alloc_m(ph):
            vaug = sb(ph, "vaug", [128, 8, 4, 257], BF16)
            kT = sb(ph, "kT", [128, 4, T], BF16)
            ktm = sb(ph, "ktm", [128, 8, 4, 128], BF16)
            qT = sb(ph, "qT", [128, 4, T], BF16)
            gi = sb(ph, "gi", [128, 32], F32)
            gf = sb(ph, "gf", [128, 32], F32)
            bq = sb(ph, "bq", [128, 32], F32)
            grep = sb(ph, "grep", [128, 32], F32)
            apr = sb(ph, "apr", [128, 32], F32)
            mx = sb(ph, "mx", [128, 32], F32)
            cm = sb(ph, "cm", [128, 32], F32)
            Rs = sb(ph, "Rs", [128, 32], F32)
            tmp32 = sb(ph, "tmp32", [128, 32], F32)
            wts = sb(ph, "wts", [128, 32], F32)
            mseg = sb(ph, "mseg", [128, 4], F32)
            return (vaug, kT, ktm, qT, gi, gf, bq, grep, apr, mx, cm, Rs, tmp32, wts, mseg)

        def m1_pass(Bf, prev_i):
            (vaug, kT, ktm, qT, gi, gf, bq, grep, apr, mx, cm, Rs, tmp32, wts, mseg) = Bf
            with ExitStack() as p1:
                pre = [sb(p1, f"pre{i}", [128, TH], F32) for i in range(2)]
                acc = [sb(p1, f"acc{i}", [128, T], F32) for i in range(2)]
                dA = [sb(p1, f"dA{i}", [128, 4, 128], F32) for i in range(2)]
                tA = [sb(p1, f"tA{i}", [128, 4, 128], F32) for i in range(2)]
                wv = [sb(p1, f"wv{i}", [128, 4, 257], BF16) for i in range(2)]
                S.op("dve", lambda e: e.memset(vaug[:].rearrange("p a b c -> p (a b c)"), 1.0), writes=["vaug"])
                pg = pp[0]
                ppi[0] = 1
                for c in range(8):
                    mm_group(pg[:, c * 8:(c + 1) * 8], "pp0", lambda k: xT[:, k, H + c * 128:H + (c + 1) * 128],
                             lambda k: wg[:, k, :], ["wg", "xT"])
                def conv_proj(o_col, dst, dkey, cbase, scale):
                    for t in range(2):
                        wb, wk = load_w([(win_d, o_col + t * 256, 256)])
                        for mi in range(2):
                            cc = t * 2 + mi
                            pr = pre[cc % 2]
                            pk = f"pre{cc % 2}"
                            ac = acc[cc % 2]
                            ak = f"acc{cc % 2}"

                            def ev_pre(_mi, si, ps, pkey, t0, n):
                                S.copy(pr[:, t0:t0 + n], ps, reads=[pkey], writes=[pk])
                            yield from proj_fm_g(wb, wk, [mi * 128], xT, "xT", SEGS_H, ev_pre)
                            wcol = 24 + (cbase + cc) * 4
                            S.op("dve", lambda e: e.tensor_scalar(out=ac[:], in0=pr[:, H - 3:H - 3 + T],
                                                                  scalar1=cb[:, wcol:wcol + 1], scalar2=None, op0=ALU.mult),
                                 reads=[pk, "cb"], writes=[ak])
                            for j in range(1, 4):
                                S.op("dve", lambda e: e.scalar_tensor_tensor(out=ac[:], in0=pr[:, H - 3 + j:H - 3 + j + T],
                                                                             scalar=cb[:, wcol + j:wcol + j + 1], in1=ac[:],
                                                                             op0=ALU.mult, op1=ALU.add),
                                     reads=[pk, "cb", ak], writes=[ak])
                            bcol = 56 + cbase + cc
                            if scale is None:
                                S.op("act", lambda e: e.activation(out=dst[:, cc, :], in_=ac[:], func=AF.Silu,
                                                                   bias=cb[:, bcol:bcol + 1]),
                                     reads=[ak, "cb"], writes=[dkey])
                            else:
                                S.op("act", lambda e: e.activation(out=ac[:], in_=ac[:], func=AF.Silu,
                                                                   bias=cb[:, bcol:bcol + 1]),
                                     reads=[ak, "cb"], writes=[ak])
                                S.op("dve", lambda e: e.tensor_scalar(out=dst[:, cc, :], in0=ac[:], scalar1=scale,
                                                                      scalar2=None, op0=ALU.mult),
                                     reads=[ak], writes=[dkey])

                def m1_bg_g():
                    for t in range(4):
                        wb, wk = load_w([(win_d, O_MV + t * 256, 256)])

                        def ev_mv2(ti, ps, pkey, t=t):
                            S.copy(vaug[:, ti, t, 0:256], ps, reads=[pkey], writes=["vaug"])
                        yield from proj_tm_g(wb, wk, 256, xT, "xT", [H + i * 128 for i in range(8)], ev_mv2)
                    yield from conv_proj(O_MK, kT, "kT", 4, None)
                    if prev_i is None:
                        yield from conv_proj(O_MQ, qT, "qT", 0, 128.0 ** -0.5)
                pump.set(m1_bg_g())
                pg3 = pg[:, 0:64].rearrange("p (c e) -> p c e", e=8)
                S.op("dve", lambda e: e.tensor_tensor(out=gi[:].rearrange("p (c h) -> p c h", h=4), in0=pg3[:, :, 0:4],
                                                      in1=cb[:, 16:20].unsqueeze(1).broadcast_to([128, 8, 4]), op=ALU.add),
                     reads=["pp0", "cb"], writes=["gi"])
                S.op("dve", lambda e: e.tensor_tensor(out=gf[:].rearrange("p (c h) -> p c h", h=4), in0=pg3[:, :, 4:8],
                                                      in1=cb[:, 20:24].unsqueeze(1).broadcast_to([128, 8, 4]), op=ALU.add),
                     reads=["pp0", "cb"], writes=["gf"])
                S.op("act", lambda e: e.activation(out=gf[:], in_=gf[:], func=AF.Sigmoid), reads=["gf"], writes=["gf"])
                S.op("act", lambda e: e.activation(out=gf[:], in_=gf[:], func=AF.Ln), reads=["gf"], writes=["gf"])
                S.op("pe", lambda e: e.matmul(pa[0][:, 0:32], lhsT=Umat, rhs=gf[:], start=True, stop=True),
                     reads=["mats", "gf"], writes=["pa0"])
                S.op("pe", lambda e: e.matmul(pa[0][:, 32:64], lhsT=ones, rhs=gf[:], start=True, stop=True),
                     reads=["mats", "gf"], writes=["pa0"])
                S.op("dve", lambda e: e.tensor_copy(out=bq[:], in_=pa[0][:, 0:32]), reads=["pa0"], writes=["bq"])
                S.op("dve", lambda e: e.tensor_copy(out=grep[:], in_=pa[0][:, 32:64]), reads=["pa0"], writes=["grep"])
                pump(4)
                S.op("dve", lambda e: e.tensor_tensor(out=apr[:], in0=gi[:], in1=bq[:], op=ALU.subtract),
                     reads=["gi", "bq"], writes=["apr"])
                pump(4)
                for c in range(8):
                    s2 = c % 2
                    S.op("dve", lambda e: e.tensor_tensor(out=dA[s2][:], in0=ident.unsqueeze(1).broadcast_to([128, 4, 128]),
                                                          in1=apr[:, c * 4:(c + 1) * 4].unsqueeze(2).broadcast_to([128, 4, 128]),
                                                          op=ALU.mult),
                         reads=["mats", "apr"], writes=[f"dA{s2}"])
                    S.op("pe", lambda e: e.matmul(pa[1][:, :], lhsT=ones, rhs=dA[s2][:].rearrange("p a b -> p (a b)"),
                                                  start=True, stop=True),
                         reads=["mats", f"dA{s2}"], writes=["pa1"])
                    pav = pa[1][:, :].rearrange("p (a b) -> p a b", a=4)
                    S.op("dve", lambda e: e.tensor_reduce(out=mx[:, c * 4:(c + 1) * 4], in_=pav, axis=AX.X, op=ALU.max),
                         reads=["pa1"], writes=["mx"])
                    S.op("dve", lambda e: e.tensor_tensor(out=tA[s2][:], in0=pav,
                                                          in1=cmask.unsqueeze(1).broadcast_to([128, 4, 128]), op=ALU.add),
                         reads=["pa1", "mats"], writes=[f"tA{s2}"])
                    S.op("dve", lambda e: e.tensor_reduce(out=cm[:, c * 4:(c + 1) * 4], in_=tA[s2][:], axis=AX.X, op=ALU.max),
                         reads=[f"tA{s2}"], writes=["cm"])
                    pump(6)
                S.op("dve", lambda e: e.tensor_copy(out=Rs[:, 28:32], in_=grep[:, 28:32]), reads=["grep"], writes=["Rs"])
                for c in range(6, -1, -1):
                    S.op("dve", lambda e: e.tensor_tensor(out=Rs[:, c * 4:(c + 1) * 4], in0=Rs[:, (c + 1) * 4:(c + 2) * 4],
                                                          in1=grep[:, c * 4:(c + 1) * 4], op=ALU.add),
                         reads=["Rs", "grep"], writes=["Rs"])
                S.op("dve", lambda e: e.tensor_tensor(out=tmp32[:], in0=Rs[:], in1=mx[:], op=ALU.add),
                     reads=["Rs", "mx"], writes=["tmp32"])
                S.op("dve", lambda e: e.tensor_reduce(out=mseg[:], in_=tmp32[:].rearrange("p (c h) -> p h c", h=4),
                                                      axis=AX.X, op=ALU.max), reads=["tmp32"], writes=["mseg"])
                S.op("dve", lambda e: e.tensor_tensor(out=tmp32[:], in0=apr[:], in1=Rs[:], op=ALU.add),
                     reads=["apr", "Rs"], writes=["tmp32"])
                S.op("dve", lambda e: e.tensor_tensor(out=tmp32[:].rearrange("p (c h) -> p c h", h=4),
                                                      in0=tmp32[:].rearrange("p (c h) -> p c h", h=4),
                                                      in1=mseg[:].unsqueeze(1).broadcast_to([128, 8, 4]), op=ALU.subtract),
                     reads=["tmp32", "mseg"], writes=["tmp32"])
                S.op("act", lambda e: e.activation(out=wts[:], in_=tmp32[:], func=AF.Exp), reads=["tmp32"], writes=["wts"])

                pump.drain()
                for c in range(8):
                    for hh in range(4):
                        s2 = (c * 4 + hh) % 2
                        pb = psb[:, s2 * 256:s2 * 256 + 128]
                        S.op("pe", lambda e: e.transpose(out=pb, in_=kT[:, hh, c * 128:(c + 1) * 128], identity=identb[:]),
                             reads=["kT", "identb"], writes=["psb"])
                        S.copy(ktm[:, c, hh, :], pb, reads=["psb"], writes=["ktm"])
                if prev_i is not None:
                    for c in range(8):
                        s2 = c % 2
                        S.op("dve", lambda e: e.tensor_tensor(out=wv[s2][:], in0=vaug[:, c, :, :],
                                                              in1=wts[:, c * 4:(c + 1) * 4].unsqueeze(2).broadcast_to([128, 4, 257]),
                                                              op=ALU.mult),
                             reads=["vaug", "wts"], writes=[f"wv{s2}"])
                        for hh in range(4):
                            S.op("pe", lambda e: e.matmul(pp[hh][:, 0:257], lhsT=ktm[:, c, hh, :], rhs=wv[s2][:, hh, :],
                                                          start=(c == 0), stop=(c == 7)),
                                 reads=["ktm", f"wv{s2}"], writes=[f"pp{hh}"], inc=True)
                    for hh in range(4):
                        S.copy(sums[:, prev_i, hh * 257:(hh + 1) * 257], pp[hh][:, 0:257], reads=[f"pp{hh}"], writes=[f"sums{prev_i}"])
                    S.op("dve", lambda e: e.tensor_copy(out=sums[:, prev_i, 1028:1032], in_=mseg[:]), reads=["mseg"], writes=[f"sums{prev_i}"])
                    S.op("dve", lambda e: e.tensor_copy(out=sums[:, prev_i, 1032:1036], in_=Rs[:, 0:4]), reads=["Rs"], writes=[f"sums{prev_i}"])
                S.barrier()

        def load_x(src):
            for g in range(8):
                S.dma("pool", xT[:, g * 4:(g + 1) * 4, :], src[:, g * 4:(g + 1) * 4, :],
                      writes=[f"xT{g}"], sem=f"xT{g}")


        for pi in range(3):
            load_x(xprev_d[pi])
            with ExitStack() as ph:
                m1_pass(alloc_m(ph), pi)
        load_x(xT_d)
        stop_if("P")

        S.barrier(("pe", "act", "dve", "sp", "pool"))
        ph23 = ExitStack()
        mkT = sb(ph23, "mkT", [128, 8, 256], BF16)
        mv = sb(ph23, "mv", [128, 2, 1024], BF16)
        memT = sb(ph23, "memT", [128, 32, 256], BF16)
        S.dma("pool", memT[:], memT_d, writes=["memT"], sem="memT")
        with ExitStack() as ph:
            akT = sb(ph, "akT", [128, 4, TH], BF16)
            av = sb(ph, "av", [128, 9, 512], BF16)
            bm = sb(ph, "bm", [128, 16, 256], F32)
            masks = sb(ph, "masks", [128, 2, 256], F32)
            qh = [sb(ph, f"qh{i}", [128, T], BF16) for i in range(2)]
            zh = [sb(ph, f"zh{i}", [128, T], BF16) for i in range(2)]
            ao = [sb(ph, f"ao{i}", [128, T], BF16) for i in range(2)]
            L = attn_scratch(ph)
            S.dma("sp", bm[:], bias_d, writes=["bm"], sem="c_bm")
            S.dma("sp", masks[:], masks_d, writes=["masks"], sem="c_masks")
            S.op("dve", lambda e: e.tensor_tensor(out=bm[:], in0=bm[:],
                                                  in1=masks[:, 0, :].unsqueeze(1).broadcast_to([128, 16, 256]),
                                                  op=ALU.add),
                 reads=["bm", "masks"], writes=["bm"])
            for t in range(2):
                wb, wk = load_w([(win_d, O_AK + t * 256, 256)])

                def ev_k(mi, si, ps, pkey, t0, n, t=t):
                    S.copy(akT[:, t * 2 + mi, t0:t0 + n], ps, reads=[pkey], writes=["akT"])
                proj_fm(wb, wk, [0, 128], xT, "xT", SEGS_H, ev_k)
            for t in range(2):
                wb, wk = load_w([(win_d, O_AV + t * 256, 256)])

                def ev_v(ti, ps, pkey, t=t):
                    S.copy(av[:, ti, t * 256:(t + 1) * 256], ps, reads=[pkey], writes=["av"])
                proj_tm(wb, wk, 256, xT, "xT", [i * 128 for i in range(9)], ev_v)
            def ld_head(h):
                return load_w([(win_d, O_AQ + h * 128, 128), (win_d, O_AZ + h * 128, 128)])

            def head_proj_g(h, wbk):
                r = h % 2
                wb, wk = wbk

                def ev_qz(mi, si, ps, pkey, t0, n):
                    if mi == 0:
                        S.op("dve", lambda e: e.tensor_scalar(out=qh[r][:, t0 - H:t0 - H + n], in0=ps,
                                                              scalar1=128.0 ** -0.5, scalar2=None, op0=ALU.mult),
                             reads=[pkey], writes=[f"qh{r}"])
                    else:
                        S.op("act", lambda e: e.activation(out=zh[r][:, t0 - H:t0 - H + n], in_=ps, func=AF.Silu),
                             reads=[pkey], writes=[f"zh{r}"])
                yield from proj_fm_g(wb, wk, [0, 128], xT, "xT", SEGS, ev_qz)

            run(head_proj_g(0, ld_head(0)))
            jobs = []
            for h in range(1, 16):
                jobs.append(("head", h))
                if h <= 8:
                    jobs.append(("kv", h - 1))
            jw = {}
            hdone = [1]
            limit = [0]

            def ldj(i):
                kind, idx = jobs[i]
                if kind == "head":
                    jw[i] = ld_head(idx)
                elif idx < 4:
                    jw[i] = load_w([(wkv_d, idx * 256, 256)])
                else:
                    jw[i] = load_w([(wkv_d, 1024 + (idx - 4) * 256, 256)])

            def jobs_g():
                for i, (kind, idx) in enumerate(jobs):
                    while i >= limit[0]:
                        yield
                    wb, wk = jw[i]
                    if kind == "head":
                        yield from head_proj_g(idx, jw[i])
                        hdone[0] = idx + 1
                    elif idx < 4:
                        def ev_mk(mi, si, ps, pkey, t0, n, t=idx):
                            S.copy(mkT[:, t * 2 + mi, :], ps, reads=[pkey], writes=["mkT"])
                        yield from proj_fm_g(wb, wk, [0, 128], memT, "memT", [(0, 256)], ev_mk)
                    else:
                        def ev_mv(ti, ps, pkey, t=idx - 4):
                            S.copy(mv[:, ti, t * 256:(t + 1) * 256], ps, reads=[pkey], writes=["mv"])
                        yield from proj_tm_g(wb, wk, 256, memT, "memT", [0, 128], ev_mv)
                    if i + 2 < len(jobs):
                        ldj(i + 2)

            ldj(0)
            ldj(1)
            pump.set(jobs_g())
            for h in range(16):
                g = h // 4
                r = h % 2
                lim = [i for i, (k, idx) in enumerate(jobs) if k == "head" and idx == h + 2]
                limit[0] = lim[0] if lim else len(jobs)
                while hdone[0] < h + 1:
                    pump(1)

                def stA(n):
                    slot = n % 2
                    S.op("pe", lambda e: e.matmul(pa[slot][:, 0:256], lhsT=qh[r][:, n * 128:(n + 1) * 128],
                                                  rhs=akT[:, g, n * 128:n * 128 + 256], start=True, stop=True),
                         reads=[f"qh{r}", "akT"], writes=[f"pa{slot}"])
                    sm_part1(L, slot, pa[slot][:, 0:256], f"pa{slot}", bm[:, h, :], ["bm"],
                             masks[:, 1, :] if n == 0 else None, cb[:, h:h + 1])

                stA(0)
                for n in range(8):
                    if n + 1 < 8:
                        stA(n + 1)
                    sm_part2(L, n % 2, lambda dc, hf: av[:, n + hf, g * 128:(g + 1) * 128], ["av"],
                             lambda dc: zh[r][:, n * 128:(n + 1) * 128], [f"zh{r}"],
                             lambda dc: ao[r][:, n * 128:(n + 1) * 128], [f"ao{r}"], 1, 2)
                S.dma("sp", mix_d[:, h, :], ao[r][:], reads=[f"ao{r}"], sem=f"st_ao{r}")
            limit[0] = len(jobs)
            pump.drain()
            S.barrier(("pe", "act", "dve", "sp", "pool"))
            stop_if("S2")

        with ExitStack() as ph:
            cq = [sb(ph, f"cq{i}", [128, 2, T], BF16) for i in range(2)]
            cz = [sb(ph, f"cz{i}", [128, 2, T], BF16) for i in range(2)]
            co = [sb(ph, f"co{i}", [128, 2, T], BF16) for i in range(2)]
            L = attn_scratch(ph)
            jobs = []
            for h in range(4):
                jobs.append(("q", h, O_CQ + h * 256))
                jobs.append(("z", h, O_CZ + h * 256))
            jw = {}
            jdone = [0]

            def ldj(i):
                jw[i] = load_w([(win_d, jobs[i][2], 256)])

            def jobs_g():
                for i, (kind, h, col) in enumerate(jobs):
                    r = h % 2
                    wb, wk = jw[i]
                    if kind == "q":
                        def ev(mi, si, ps, pkey, t0, n, r=r):
                            S.op("dve", lambda e: e.tensor_scalar(out=cq[r][:, mi, t0 - H:t0 - H + n], in0=ps,
                                                                  scalar1=256.0 ** -0.5, scalar2=None, op0=ALU.mult),
                                 reads=[pkey], writes=[f"cq{r}"])
                    else:
                        def ev(mi, si, ps, pkey, t0, n, r=r):
                            S.op("act", lambda e: e.activation(out=cz[r][:, mi, t0 - H:t0 - H + n], in_=ps, func=AF.Silu),
                                 reads=[pkey], writes=[f"cz{r}"])
                    yield from proj_fm_g(wb, wk, [0, 128], xT, "xT", SEGS, ev)
                    jdone[0] = i + 1
                    if i + 2 < len(jobs):
                        ldj(i + 2)

            ldj(0)
            ldj(1)
            pump.set(jobs_g())
            for h in range(4):
                r = h % 2
                while jdone[0] < 2 * h + 2:
                    pump(1)

                def stA(n):
                    slot = n % 2
                    for dc in range(2):
                        S.op("pe", lambda e: e.matmul(pa[slot][:, 0:256], lhsT=cq[r][:, dc, n * 128:(n + 1) * 128],
                                                      rhs=mkT[:, h * 2 + dc, :], start=(dc == 0), stop=(dc == 1)),
                             reads=[f"cq{r}", "mkT"], writes=[f"pa{slot}"], inc=(dc == 1))
                    sm_part1(L, slot, pa[slot][:, 0:256], f"pa{slot}", None, [], None, None)

                stA(0)
                for n in range(8):
                    if n + 1 < 8:
                        stA(n + 1)
                    sm_part2(L, n % 2, lambda dc, hf: mv[:, hf, h * 256 + dc * 128:h * 256 + (dc + 1) * 128], ["mv"],
                             lambda dc: cz[r][:, dc, n * 128:(n + 1) * 128], [f"cz{r}"],
                             lambda dc: co[r][:, dc, n * 128:(n + 1) * 128], [f"co{r}"], 2, 2)
                S.dma("sp", mix_d[:, 24 + h * 2:26 + h * 2, :], co[r][:], reads=[f"co{r}"],
                      sem=f"st_co{r}")
            pump.drain()
            S.barrier()
            stop_if("S3")
        ph23.close()

        with ExitStack() as ph:
            Bf = alloc_m(ph)
            (vaug, kT, ktm, qT, gi, gf, bq, grep, apr, mx, cm, Rs, tmp32, wts, mseg) = Bf
            m1_pass(Bf, None)
            stop_if("M1")

            with ExitStack() as p2:
                Cst = sb(p2, "Cst", [128, 4, 257], F32)
                Cbf = sb(p2, "Cbf", [128, 4, 257], BF16)
                mst = sb(p2, "mst", [128, 4], F32)
                sc = [sb(p2, f"sc{i}", [128, 8, 4], F32) for i in range(2)]
                og = sb(p2, "og", [128, 8, 256], BF16)
                zg = sb(p2, "zg", [128, 8, 256], BF16)
                gvec = sb(p2, "gvec", [128, 1024], F32)
                moT = [sb(p2, "moT0", [128, 2, T], BF16)] * 2
                alloc_sg(p2, 256)
                Mv = sb(p2, "Mv", [128, 32], F32)
                mcs = sb(p2, "mcs", [128, 9, 4], F32)
                Ml = sb(p2, "Ml", [128, 32], F32)
                wint = sb(p2, "wint", [128, 32], F32)
                spv = sb(p2, "spv", [128, 32], F32)
                w2 = sb(p2, "w2", [128, 32], F32)
                emt = sb(p2, "emt", [128, 32], F32)
                dM = [sb(p2, f"dM{i}", [128, 128], F32) for i in range(2)]
                ET = [sb(p2, f"ET{i}", [128, 128], F32) for i in range(2)]
                Sm = [sb(p2, f"Sm{i}", [128, 128], BF16) for i in range(2)]
                it = [sb(p2, f"it{i}", [128, 257], F32) for i in range(2)]
                oa = [sb(p2, f"oa{i}", [128, 257], F32) for i in range(2)]
                hs = [sb(p2, f"hs{i}", [128, 256], F32) for i in range(2)]
                mo = [sb(p2, f"mo{i}", [128, 256], BF16) for i in range(2)]
                wv2 = [sb(p2, f"wv2{i}", [128, 257], BF16) for i in range(2)]
                s6 = [sb(p2, f"s6{i}", [128, 16], F32) for i in range(2)]
                S.dma("sp", gvec[:], gv_d, writes=["gvec"], sem="c_gv")
                S.op("dve", lambda e: e.memset(Cst[:].rearrange("p a b -> p (a b)"), 0.0), writes=["Cst"])
                S.op("dve", lambda e: e.memset(mst[:], 0.0), writes=["mst"])
                for rnk in range(3):
                    a2 = rnk % 2
                    ab, abk = sums[:, rnk, :], f"sums{rnk}"
                    s_, sk = sc[a2], f"sc{a2}"
                    mcol = cb[:, 64 + rnk:65 + rnk]
                    ncol = cb[:, 72 + rnk:73 + rnk]
                    S.op("dve", lambda e: e.tensor_scalar(out=s_[:, 0, :], in0=ab[:, 1032:1036], scalar1=mcol, scalar2=None,
                                                          op0=ALU.mult), reads=[abk, "cb"], writes=[sk])
                    S.op("dve", lambda e: e.tensor_scalar(out=s_[:, 1, :], in0=ab[:, 1028:1032], scalar1=mcol, scalar2=ncol,
                                                          op0=ALU.mult, op1=ALU.add), reads=[abk, "cb"], writes=[sk])
                    S.op("dve", lambda e: e.tensor_tensor(out=s_[:, 2, :], in0=s_[:, 0, :], in1=mst[:], op=ALU.add),
                         reads=[sk, "mst"], writes=[sk])
                    S.op("dve", lambda e: e.tensor_tensor(out=s_[:, 3, :], in0=s_[:, 2, :], in1=s_[:, 1, :], op=ALU.max),
                         reads=[sk], writes=[sk])
                    S.op("dve", lambda e: e.tensor_tensor(out=s_[:, 4, :], in0=s_[:, 2, :], in1=s_[:, 3, :], op=ALU.subtract),
                         reads=[sk], writes=[sk])
                    S.op("dve", lambda e: e.tensor_tensor(out=s_[:, 5, :], in0=s_[:, 1, :], in1=s_[:, 3, :], op=ALU.subtract),
                         reads=[sk], writes=[sk])
                    S.op("act", lambda e: e.activation(out=s_[:, 4:6, :], in_=s_[:, 4:6, :], func=AF.Exp), reads=[sk], writes=[sk])
                    for hh in range(4):
                        S.op("dve", lambda e: e.tensor_scalar(out=Cst[:, hh, :], in0=Cst[:, hh, :], scalar1=s_[:, 4, hh:hh + 1],
                                                              scalar2=None, op0=ALU.mult), reads=["Cst", sk], writes=["Cst"])
                        S.op("dve", lambda e: e.scalar_tensor_tensor(out=Cst[:, hh, :], in0=ab[:, hh * 257:(hh + 1) * 257],
                                                                     scalar=s_[:, 5, hh:hh + 1], in1=Cst[:, hh, :],
                                                                     op0=ALU.mult, op1=ALU.add),
                             reads=["Cst", sk, abk], writes=["Cst"])
                    S.op("dve", lambda e: e.tensor_copy(out=mst[:], in_=s_[:, 3, :]), reads=[sk], writes=["mst"])
                S.op("dve", lambda e: e.tensor_copy(out=mcs[:, 0, :], in_=mst[:]), reads=["mst"], writes=["mcs"])
                for c in range(8):
                    S.op("dve", lambda e: e.tensor_tensor(out=Ml[:, c * 4:(c + 1) * 4], in0=mcs[:, c, :],
                                                          in1=mx[:, c * 4:(c + 1) * 4], op=ALU.max),
                         reads=["mcs", "mx"], writes=["Ml"])
                    S.op("dve", lambda e: e.tensor_tensor(out=mcs[:, c + 1, :], in0=Ml[:, c * 4:(c + 1) * 4],
                                                          in1=grep[:, c * 4:(c + 1) * 4], op=ALU.add),
                         reads=["Ml", "grep"], writes=["mcs"])
                mc32 = mcs[:, 0:8, :].rearrange("p c h -> p (c h)")
                S.op("dve", lambda e: e.tensor_tensor(out=Mv[:], in0=mc32, in1=cm[:], op=ALU.max),
                     reads=["mcs", "cm"], writes=["Mv"])
                S.op("dve", lambda e: e.tensor_tensor(out=wint[:], in0=mc32, in1=Mv[:], op=ALU.subtract),
                     reads=["mcs", "Mv"], writes=["wint"])
                S.op("act", lambda e: e.activation(out=wint[:], in_=wint[:], func=AF.Exp), reads=["wint"], writes=["wint"])
                S.op("dve", lambda e: e.tensor_tensor(out=spv[:], in0=mc32, in1=Ml[:], op=ALU.subtract),
                     reads=["mcs", "Ml"], writes=["spv"])
                S.op("act", lambda e: e.activation(out=spv[:], in_=spv[:], func=AF.Exp), reads=["spv"], writes=["spv"])
                S.op("dve", lambda e: e.tensor_tensor(out=w2[:], in0=apr[:], in1=Ml[:], op=ALU.subtract),
                     reads=["apr", "Ml"], writes=["w2"])
                S.op("act", lambda e: e.activation(out=w2[:], in_=w2[:], func=AF.Exp), reads=["w2"], writes=["w2"])
                S.op("dve", lambda e: e.tensor_tensor(out=emt[:], in0=bq[:], in1=Mv[:], op=ALU.add),
                     reads=["bq", "Mv"], writes=["emt"])
                S.op("act", lambda e: e.activation(out=emt[:], in_=emt[:], func=AF.Exp, scale=-1.0), reads=["emt"], writes=["emt"])

                TOFF = [H + i * 128 for i in range(8)]
                jw2 = {}
                tdone = [0]
                blim = [8]
                bgi = [0]

                def ld2(h):
                    jw2[(h, "o")] = load_w([(win_d, O_MO + h * 256, 256)])
                    jw2[(h, "z")] = load_w([(win_d, O_MZ + h * 256, 256)])

                def bg_tile_g(wbk, t0, dst, dkey, func):
                    wb, wk = wbk
                    half = bgi[0] % 2
                    bgi[0] += 1
                    ps = pa[1][:, half * 256:(half + 1) * 256]
                    yield from mm_group_g(ps, "pa1", lambda k: xT[:, k, t0:t0 + 128], lambda k: wb[:, k, 0:256],
                                          [wk, "xT"])
                    gate_evac(func, dst, dkey, ps, "pa1", 256)
                    yield

                def bg_g():
                    for h in range(4):
                        for ti in range(8):
                            while h * 8 + ti >= blim[0]:
                                yield
                            yield from bg_tile_g(jw2[(h, "o")], TOFF[ti], og[:, ti, :], f"og{ti}", "sigmoid")
                            yield from bg_tile_g(jw2[(h, "z")], TOFF[ti], zg[:, ti, :], f"zg{ti}", "silu")
                            tdone[0] += 1
                        if h + 1 < 4:
                            ld2(h + 1)

                ld2(0)
                pump.set(bg_g())

                def fpu(hh, c):
                    i2 = c % 2
                    ci = c * 4 + hh
                    S.op("dve", lambda e: e.tensor_scalar(out=dM[i2][:], in0=ident, scalar1=Mv[:, ci:ci + 1], scalar2=None,
                                                          op0=ALU.mult), reads=["mats", "Mv"], writes=[f"dM{i2}"])
                    pm = pa[0][:, 0:128]
                    S.op("pe", lambda e: e.matmul(pm, lhsT=ones, rhs=dM[i2][:], start=True, stop=False),
                         reads=["mats", f"dM{i2}"], writes=["pa0"], inc=False)
                    S.op("pe", lambda e: e.matmul(pm, lhsT=ident, rhs=cmaskT, start=False, stop=True),
                         reads=["mats"], writes=["pa0"])
                    S.op("act", lambda e: e.activation(out=ET[i2][:], in_=pm, func=AF.Exp, bias=apr[:, ci:ci + 1], scale=-1.0),
                         reads=["pa0", "apr"], writes=[f"ET{i2}"])
                    pump(1)
                    pst = pa[0][:, 128:256]
                    S.op("pe", lambda e: e.matmul(pst, lhsT=kT[:, hh, c * 128:(c + 1) * 128], rhs=qT[:, hh, c * 128:(c + 1) * 128],
                                                  start=True, stop=True), reads=["kT", "qT"], writes=["pa0"])
                    S.op("dve", lambda e: e.tensor_tensor(out=Sm[i2][:], in0=pst, in1=ET[i2][:], op=ALU.mult),
                         reads=["pa0", f"ET{i2}"], writes=[f"Sm{i2}"])
                    pump(1)
                    pin = pp[i2 * 2]
                    pink = f"pp{i2 * 2}"
                    pit = pp[i2 * 2 + 1]
                    pitk = f"pp{i2 * 2 + 1}"
                    S.op("pe", lambda e: e.matmul(pin[:, 0:257], lhsT=Sm[i2][:], rhs=vaug[:, c, hh, :], start=True, stop=True),
                         reads=[f"Sm{i2}", "vaug"], writes=[pink])
                    S.op("pe", lambda e: e.matmul(pit[:, 0:257], lhsT=qT[:, hh, c * 128:(c + 1) * 128], rhs=Cbf[:, hh, :],
                                                  start=True, stop=True), reads=["qT", "Cbf"], writes=[pitk])
                    if c < 7:
                        S.op("dve", lambda e: e.tensor_scalar(out=wv2[i2][:], in0=vaug[:, c, hh, :], scalar1=w2[:, ci:ci + 1],
                                                              scalar2=None, op0=ALU.mult),
                             reads=["vaug", "w2"], writes=[f"wv2{i2}"])
                        pump(1)
                        pdc = po[:, 0:257]
                        S.op("pe", lambda e: e.matmul(pdc, lhsT=ktm[:, c, hh, :], rhs=wv2[i2][:], start=True, stop=True),
                             reads=["ktm", f"wv2{i2}"], writes=["po"])
                        S.op("dve", lambda e: e.scalar_tensor_tensor(out=Cst[:, hh, :], in0=Cst[:, hh, :],
                                                                     scalar=spv[:, ci:ci + 1], in1=pdc,
                                                                     op0=ALU.mult, op1=ALU.add),
                             reads=["Cst", "spv", "po"], writes=["Cst"])
                        S.op("act", lambda e: e.copy(out=Cbf[:, hh, :], in_=Cst[:, hh, :]), reads=["Cst"], writes=["Cbf"])

                def back(hh, c):
                    r = hh % 2
                    i2 = c % 2
                    ci = c * 4 + hh
                    pin = pp[i2 * 2]
                    pink = f"pp{i2 * 2}"
                    pit = pp[i2 * 2 + 1]
                    pitk = f"pp{i2 * 2 + 1}"
                    S.op("act", lambda e: e.mul(out=it[i2][:], in_=pit[:, 0:257], mul=wint[:, ci:ci + 1]),
                         reads=[pitk, "wint"], writes=[f"it{i2}"])
                    yield
                    S.op("dve", lambda e: e.tensor_tensor(out=oa[i2][:], in0=pin[:, 0:257], in1=it[i2][:], op=ALU.add),
                         reads=[pink, f"it{i2}"], writes=[f"oa{i2}"])
                    yield
                    pump(1)
                    q6 = s6[i2]
                    qk = f"s6{i2}"
                    S.op("act", lambda e: e.activation(out=q6[:, 0:1], in_=oa[i2][:, 256:257], func=AF.Abs),
                         reads=[f"oa{i2}"], writes=[qk])
                    yield
                    S.op("dve", lambda e: e.tensor_tensor(out=q6[:, 1:2], in0=q6[:, 0:1], in1=emt[:, ci:ci + 1], op=ALU.max),
                         reads=[qk, "emt"], writes=[qk])
                    yield
                    S.op("dve", lambda e: e.reciprocal(out=q6[:, 2:3], in_=q6[:, 1:2]), reads=[qk], writes=[qk])
                    yield
                    pump(1)
                    S.op("dve", lambda e: e.scalar_tensor_tensor(out=hs[i2][:], in0=oa[i2][:, 0:256], scalar=q6[:, 2:3],
                                                                 in1=og[:, c, :], op0=ALU.mult, op1=ALU.mult),
                         reads=[f"oa{i2}", qk, f"og{c}"], writes=[f"hs{i2}"])
                    yield
                    S.op("dve", lambda e: e.bn_stats(out=q6[:, 4:10], in_=hs[i2][:]), reads=[f"hs{i2}"], writes=[qk])
                    yield
                    S.op("dve", lambda e: e.bn_aggr(out=q6[:, 10:12], in_=q6[:, 4:10]), reads=[qk], writes=[qk])
                    yield
                    pump(1)
                    S.op("act", lambda e: e.activation(out=q6[:, 12:13], in_=q6[:, 11:12], func=AF.Ln, bias=cb[:, 81:82]),
                         reads=[qk, "cb"], writes=[qk])
                    yield
                    S.op("act", lambda e: e.activation(out=q6[:, 13:14], in_=q6[:, 12:13], func=AF.Exp, scale=-0.5),
                         reads=[qk], writes=[qk])
                    yield
                    pump(1)
                    S.op("dve", lambda e: e.tensor_scalar(out=hs[i2][:], in0=hs[i2][:], scalar1=q6[:, 10:11],
                                                          scalar2=q6[:, 13:14], op0=ALU.subtract, op1=ALU.mult),
                         reads=[f"hs{i2}", qk], writes=[f"hs{i2}"])
                    yield
                    S.op("dve", lambda e: e.tensor_tensor(out=hs[i2][:], in0=hs[i2][:], in1=gvec[:, hh * 256:(hh + 1) * 256],
                                                          op=ALU.mult), reads=[f"hs{i2}", "gvec"], writes=[f"hs{i2}"])
                    yield
                    pump(1)
                    S.op("dve", lambda e: e.tensor_tensor(out=mo[i2][:], in0=hs[i2][:], in1=zg[:, c, :], op=ALU.mult),
                         reads=[f"hs{i2}", f"zg{c}"], writes=[f"mo{i2}"])
                    yield
                    pump(2)
                    for fc in range(2):
                        pb = psb[:, i2 * 256 + fc * 128:i2 * 256 + (fc + 1) * 128]
                        S.op("pe", lambda e: e.transpose(out=pb, in_=mo[i2][:, fc * 128:(fc + 1) * 128], identity=identb[:]),
                             reads=[f"mo{i2}", "identb"], writes=["psb"])
                    yield
                    S.op("act", lambda e: e.copy(out=moT[r][:, :, c * 128:(c + 1) * 128],
                                                 in_=psb[:, i2 * 256:(i2 + 1) * 256].rearrange("p (a b) -> p a b", a=2)),
                         reads=["psb"], writes=["moT"])
                    yield
                    pump(1)

                for hh in range(4):
                    r = hh % 2
                    while tdone[0] < hh * 8 + 1:
                        pump(1)
                    S.op("dve", lambda e: e.tensor_copy(out=Cbf[:, hh, :], in_=Cst[:, hh, :]), reads=["Cst"], writes=["Cbf"])
                    for c in range(0, 8, 2):
                        blim[0] = hh * 8 + c + 8
                        fpu(hh, c)
                        fpu(hh, c + 1)
                        while tdone[0] < min(32, hh * 8 + c + 2):
                            pump(1)
                        g0 = back(hh, c)
                        g1 = back(hh, c + 1)
                        alive = [g0, g1]
                        while alive:
                            for gg in list(alive):
                                try:
                                    next(gg)
                                except StopIteration:
                                    alive.remove(gg)
                    S.dma("sp", mix_d[:, 16 + hh * 2:18 + hh * 2, :], moT[r][:], reads=["moT"],
                          sem="st_mo")
                blim[0] = 64
                pump.drain()
                S.barrier()
                stop_if("M2")
            S.barrier()

        with ExitStack() as ph:
            mixT = xT[:].rearrange("p k t -> p (k t)")[:, 0:32 * T].rearrange("p (k t) -> p k t", k=32)
            xr = [sb(ph, f"xr{i}", [128, 256], F32) for i in range(4)]
            zt = [sb(ph, f"zt{i}", [128, 2048], F32) for i in range(4)]
            lng = sb(ph, "lng", [128, 4096], F32)
            lnb = sb(ph, "lnb", [128, 4096], F32)
            stt = sb(ph, "stt", [128, 8, 96], F32)
            mvs = sb(ph, "mvs", [128, 8, 2], F32)
            rs = sb(ph, "rs", [128, 8, 4], F32)
            S.dma("sp", lng[:], lng_d, writes=["lng"], sem="c_lng")
            S.dma("sp", lnb[:], lnb_d, writes=["lnb"], sem="c_lnb")
            for g in range(4):
                S.dma("sp", mixT[:, g * 8:(g + 1) * 8, :], mix_d[:, g * 8:(g + 1) * 8, :], writes=[f"xT{2 * g}", f"xT{2 * g + 1}"],
                      sem=f"mixl{g}")
            alpha = 2.0 ** 0.25
            xi = 0
            for t in range(16):
                wb, wk = load_w([(wout_d, t * 256, 256)])
                for c in range(8):
                    x4 = xi % 4
                    xi += 1
                    S.dma("sp", xr[x4][:], xres_d[c * 128:(c + 1) * 128, t * 256:(t + 1) * 256], writes=[f"xr{x4}"],
                          sem=f"xr{x4}")
                    ps, pkey = next_pp()
                    mm_group(ps[:, 0:256], pkey, lambda k: mixT[:, k, c * 128:(c + 1) * 128], lambda k: wb[:, k, :],
                             [wk, "xT"])
                    S.op("dve", lambda e: e.scalar_tensor_tensor(out=xr[x4][:], in0=xr[x4][:], scalar=alpha, in1=ps[:, 0:256],
                                                                 op0=ALU.mult, op1=ALU.add),
                         reads=[f"xr{x4}", pkey], writes=[f"xr{x4}"])
                    S.op("dve", lambda e: e.bn_stats(out=stt[:, c, t * 6:(t + 1) * 6], in_=xr[x4][:]),
                         reads=[f"xr{x4}"], writes=["stt"])
                    S.dma("act", zscr_d[c * 128:(c + 1) * 128, t * 256:(t + 1) * 256], xr[x4][:], reads=[f"xr{x4}"],
                          sem=f"zs{x4}")
            for c in range(8):
                S.op("dve", lambda e: e.bn_aggr(out=mvs[:, c, :], in_=stt[:, c, :]), reads=["stt"], writes=["mvs"])
            S.op("act", lambda e: e.activation(out=rs[:, :, 0], in_=mvs[:, :, 1], func=AF.Sqrt, bias=cb[:, 80:81]),
                 reads=["mvs", "cb"], writes=["rs"])
            S.op("dve", lambda e: e.reciprocal(out=rs[:, :, 1], in_=rs[:, :, 0]), reads=["rs"], writes=["rs"])
            S.op("dve", lambda e: e.scalar_tensor_tensor(out=rs[:, :, 2], in0=mvs[:, :, 0], scalar=-1.0, in1=rs[:, :, 1],
                                                         op0=ALU.mult, op1=ALU.mult), reads=["mvs", "rs"], writes=["rs"])
            S.barrier()
            for i in range(16):
                c, hf = i // 2, i % 2
                z4 = i % 4
                z, zk = zt[z4], f"zt{z4}"
                cs = slice(hf * 2048, (hf + 1) * 2048)
                S.dma("sp", z[:], zscr_d[c * 128:(c + 1) * 128, cs], writes=[zk], sem=f"zl{z4}")
                S.op("act", lambda e: e.activation(out=z[:], in_=z[:], func=AF.Identity, bias=rs[:, c, 2:3], scale=rs[:, c, 1:2]),
                     reads=[zk, "rs"], writes=[zk])
                S.op("dve", lambda e: e.tensor_tensor(out=z[:], in0=z[:], in1=lng[:, cs], op=ALU.mult),
                     reads=[zk, "lng"], writes=[zk])
                if i % 3 == 2:
                    S.op("dve", lambda e: e.tensor_tensor(out=z[:], in0=z[:], in1=lnb[:, cs], op=ALU.add),
                         reads=[zk, "lnb"], writes=[zk])
                    S.dma("act", out_d[c * 128:(c + 1) * 128, cs], z[:], reads=[zk], sem=f"zfa{z4}")
                else:
                    S.op("pool", lambda e: e.tensor_tensor(out=z[:], in0=z[:], in1=lnb[:, cs], op=ALU.add),
                         reads=[zk, "lnb"], writes=[zk])
                    S.dma("pool", out_d[c * 128:(c + 1) * 128, cs], z[:], reads=[zk], sem=f"zf{z4}")
            S.finish("sp")
    except _Stop:
        pass
    return nc


_T5 = None


def _t5_bucket(dist):
    max_exact = 16
    is_small = dist < max_exact
    ratio = np.maximum(dist, max_exact).astype(np.float32) / np.float32(max_exact)
    large = max_exact + (np.log(ratio) / np.float32(np.log(128 / max_exact)) * np.float32(16)).astype(np.int32)
    large = np.minimum(large, 31)
    return np.where(is_small, dist, large)


def _host_prep(inputs):
    f32 = np.float32
    x = np.asarray(inputs["x"], f32)
    mem = np.asarray(inputs["mem"], f32)
    w_in = np.asarray(inputs["w_in"], f32)
    w_kv = np.asarray(inputs["w_mem_kv"], f32)
    w_out = np.asarray(inputs["w_out"], f32)
    w_in_t = np.ascontiguousarray(w_in.reshape(32, 128, IN_W).transpose(1, 0, 2))
    wg_h = np.ascontiguousarray(w_in_t[:, :, O_MI:O_MI + 8])
    w_kv_t = np.ascontiguousarray(w_kv.reshape(32, 128, 2048).transpose(1, 0, 2))
    w_out_t = np.ascontiguousarray(w_out.reshape(32, 128, 4096).transpose(1, 0, 2))
    rep = lambda v: np.ascontiguousarray(np.broadcast_to(np.asarray(v, f32)[None, :], (128, len(v))))
    gvec = rep(inputs["m_norm_g"])
    lng = rep(inputs["ln_g"])
    lnb = rep(inputs["ln_b"])
    qi = np.arange(128, dtype=np.int32)[:, None]
    si = np.arange(256, dtype=np.int32)[None, :]
    dist = qi + 128 - si
    bucket = _t5_bucket(np.clip(dist, 0, 127))
    rel_bias = np.asarray(inputs["rel_bias"], f32)
    bias = np.ascontiguousarray(rel_bias[bucket].transpose(0, 2, 1))
    winmask = np.where((dist >= 0) & (dist < 128), 0.0, NEG).astype(f32)
    mats = np.zeros((128, 5, 128), f32)
    mats[:, 0] = np.eye(128)
    mats[:, 1] = 1.0
    ii = np.arange(128)
    mats[:, 2] = (ii[:, None] <= ii[None, :])
    mats[:, 3] = np.where(ii[None, :] <= ii[:, None], 0.0, NEG)
    mats[:, 4] = np.where(ii[:, None] <= ii[None, :], 0.0, -NEG)
    conv_w = np.asarray(inputs["conv_w"], f32)
    conv_b = np.asarray(inputs["conv_b"], f32)
    in_maps = []
    for c in range(8):
        b, j = c // 4, c % 4
        t0 = j * T
        seg = np.zeros((TH, 4096), f32)
        if j > 0:
            seg[:] = x[b, t0 - H:t0 + T]
        else:
            seg[H:] = x[b, 0:T]
        xT = np.ascontiguousarray(seg.T.reshape(32, 128, TH).transpose(1, 0, 2))
        xres = np.ascontiguousarray(x[b, t0:t0 + T])
        memT = np.ascontiguousarray(mem[b].T.reshape(32, 128, 256).transpose(1, 0, 2))
        cb = np.zeros((128, NCB), f32)
        cb[:, 0:16] = np.asarray(inputs["sinks"], f32)[None, :]
        cb[:, 16:20] = np.asarray(inputs["b_i"], f32)[None, :]
        cb[:, 20:24] = np.asarray(inputs["b_f"], f32)[None, :]
        cb[:, 24:56] = conv_w.reshape(4, 8, 128).transpose(2, 1, 0).reshape(128, 32)
        cb[:, 56:64] = conv_b.reshape(8, 128).T
        xprev = np.zeros((3, 128, 32, TH), f32)
        for r in range(3):
            g = j - 3 + r
            on = g >= 0
            cb[:, 64 + r] = 1.0 if on else 0.0
            cb[:, 72 + r] = 0.0 if on else NEG
            if on:
                sg = np.zeros((TH, 4096), f32)
                g0 = g * T
                if g > 0:
                    sg[:] = x[b, g0 - H:g0 + T]
                else:
                    sg[H:] = x[b, 0:T]
                xprev[r] = sg.T.reshape(32, 128, TH).transpose(1, 0, 2)
        cb[:, 80] = 1e-5
        cb[:, 81] = 1e-6
        masks = np.zeros((128, 2, 256), f32)
        masks[:, 0] = winmask
        if j == 0:
            masks[:, 1, 0:128] = NEG
        in_maps.append({"xT": xT, "xres": xres, "xprev": xprev, "memT": memT, "w_in_t": w_in_t, "w_kv_t": w_kv_t,
                        "w_out_t": w_out_t, "cb": cb, "gvec": gvec, "lng": lng, "lnb": lnb,
                        "bias": bias, "mats": mats, "masks": masks, "wg": wg_h})
    return in_maps


_NC = None


def kernel(**inputs):
    global _NC
    in_maps = _host_prep(inputs)
    if _NC is None:
        _NC = build_program()
    res = run_bass_kernel_spmd(_NC, in_maps, core_ids=list(range(8)))
    out = np.empty((2, 4096, 4096), np.float32)
    for c in range(8):
        b, j = c // 4, c % 4
        out[b, j * T:(j + 1) * T] = np.asarray(res.results[c]["out"], np.float32)
    kernel.last_results = res.results
    return out
```
